# Optimizing a Trainium2 kernel written in Bass

```python
import math
import jax, jax.numpy as jnp
from jax import lax
import numpy as np

D_MODEL = 4096
BATCH = 4
SEQ = 4096
DEPTH = 1

CHUNK = 64
MIX_WIDTH = D_MODEL
RET_WIDTH = MIX_WIDTH // 2
RET_HEADS = 8
RET_HEAD_DIM = RET_WIDTH // RET_HEADS
GDN_WIDTH = MIX_WIDTH - RET_WIDTH
GDN_HEAD_DIM = 128
GDN_HEADS = GDN_WIDTH // GDN_HEAD_DIM
CONV_WIDTH = 4
D_FF = 4 * D_MODEL
ROPE_BASE = 10000.0
EPS = 1e-6
RET_COLS = 4 * RET_WIDTH
GDN_QKV_COLS = 3 * GDN_WIDTH
GDN_COLS = GDN_QKV_COLS + GDN_WIDTH + 2 * GDN_HEADS
IN_COLS = RET_COLS + GDN_COLS

kernel_name = "hybrid_retention_gated_deltanet_block"


def _rmsnorm(x, gain):
    xf = x.astype(jnp.float32)
    y = xf * lax.rsqrt(jnp.mean(xf * xf, axis=-1, keepdims=True) + EPS)
    return (y * gain.astype(jnp.float32)).astype(x.dtype)


def _heads(t, n_heads):
    B, T, _ = t.shape
    return t.reshape(B, T, n_heads, -1).transpose(0, 2, 1, 3)


def _merge_heads(t):
    B, H, T, d = t.shape
    return t.transpose(0, 2, 1, 3).reshape(B, T, H * d)


def _rope(t):
    T, d = t.shape[2], t.shape[3]
    inv = ROPE_BASE ** (-jnp.arange(d // 2, dtype=jnp.float32) * (2.0 / d))
    ang = jnp.arange(T, dtype=jnp.float32)[:, None] * inv[None, :]
    cos = jnp.cos(ang).astype(t.dtype)
    sin = jnp.sin(ang).astype(t.dtype)
    t1, t2 = t[..., : d // 2], t[..., d // 2:]
    return jnp.concatenate([t1 * cos - t2 * sin, t1 * sin + t2 * cos], axis=-1)


def _retention(q, k, v):
    B, H, T, d = q.shape
    N = T // CHUNK
    dt = q.dtype
    log_gamma = jnp.log1p(-jnp.exp2(-5.0 - jnp.arange(H, dtype=jnp.float32)))
    pos = jnp.arange(CHUNK, dtype=jnp.float32)
    d_sym = jnp.exp(log_gamma[:, None, None] * jnp.abs(pos[:, None] - pos[None, :]))
    xi = jnp.exp(log_gamma[:, None] * (pos + 1.0))
    zeta = jnp.exp(log_gamma[:, None] * (CHUNK - 1.0 - pos))
    gamma_chunk = jnp.exp(log_gamma * CHUNK)
    q = _rope(q)
    k = _rope(k) * (d ** -0.5)
    qc, kc, vc = (t.reshape(B, H, N, CHUNK, -1) for t in (q, k, v))
    scores = jnp.einsum('bhncd,bhnmd->bhncm', qc, kc) * d_sym[:, None].astype(dt)
    o_intra = jnp.einsum('bhncm,bhnme->bhnce', scores, vc)
    k_dec = kc * zeta[:, None, :, None].astype(dt)
    xi_b = xi[None, :, :, None].astype(dt)
    gc_b = gamma_chunk[None, :, None, None].astype(dt)

    def step(R, inp):
        q_n, k_n, v_n = inp
        o = jnp.einsum('bhcd,bhde->bhce', q_n, R) * xi_b
        R = R * gc_b + jnp.einsum('bhcd,bhce->bhde', k_n, v_n)
        return R, o

    R0 = jnp.zeros((B, H, d, v.shape[-1]), dt)
    xs = (jnp.moveaxis(qc, 2, 0), jnp.moveaxis(k_dec, 2, 0), jnp.moveaxis(vc, 2, 0))
    _, o_inter = lax.scan(step, R0, xs)
    o = o_intra + jnp.moveaxis(o_inter, 0, 2)
    return o.reshape(B, H, T, -1)


def _head_groupnorm(o, gain):
    of = o.astype(jnp.float32)
    mu = jnp.mean(of, axis=-1, keepdims=True)
    var = jnp.mean(jnp.square(of - mu), axis=-1, keepdims=True)
    y = _merge_heads((of - mu) * lax.rsqrt(var + EPS))
    return (y * gain.astype(jnp.float32)).astype(o.dtype)


def _causal_conv(u, w):
    C = u.shape[-1]
    return lax.conv_general_dilated(
        u, w[:, None, :].astype(u.dtype), window_strides=(1,),
        padding=[(CONV_WIDTH - 1, 0)], dimension_numbers=('NWC', 'WIO', 'NWC'),
        feature_group_count=C)


def _gated_delta(q, k, v, g, beta):
    B, H, T, d = q.shape
    dv = v.shape[-1]
    N = T // CHUNK
    dt = q.dtype
    f32 = jnp.float32
    qc, kc, vc = (t.reshape(B, H, N, CHUNK, -1) for t in (q, k, v))
    gc = jnp.cumsum(g.reshape(B, H, N, CHUNK), axis=-1)
    bc = beta.reshape(B, H, N, CHUNK)
    idx = jnp.arange(CHUNK)
    causal = idx[:, None] >= idx[None, :]
    strict = idx[:, None] > idx[None, :]
    decay = jnp.exp(jnp.where(causal, gc[..., :, None] - gc[..., None, :], -jnp.inf))
    kf = kc.astype(f32)
    k_beta = kf * bc[..., None]
    lower = jnp.where(strict, jnp.einsum('bhncd,bhnmd->bhncm', k_beta, kf) * decay, 0.0)
    a = lower + jnp.eye(CHUNK, dtype=f32)
    rhs = jnp.concatenate([vc.astype(f32) * bc[..., None], k_beta * jnp.exp(gc)[..., None]], axis=-1)
    sol = lax.linalg.triangular_solve(a, rhs, left_side=True, lower=True, unit_diagonal=True)
    u = sol[..., :dv].astype(dt)
    w = sol[..., dv:].astype(dt)
    attn = jnp.einsum('bhncd,bhnmd->bhncm', qc, kc) * decay.astype(dt)
    q_dec = qc * jnp.exp(gc)[..., None].astype(dt)
    k_dec = kc * jnp.exp(gc[..., -1:] - gc)[..., None].astype(dt)
    chunk_decay = jnp.exp(gc[..., -1]).astype(dt)

    def step(S, inp):
        q_n, k_n, u_n, w_n, a_n, cd = inp
        v_new = u_n - jnp.einsum('bhcd,bhde->bhce', w_n, S)
        o = jnp.einsum('bhcd,bhde->bhce', q_n, S) + jnp.einsum('bhcm,bhme->bhce', a_n, v_new)
        S = S * cd[..., None, None] + jnp.einsum('bhcd,bhce->bhde', k_n, v_new)
        return S, o

    xs = tuple(jnp.moveaxis(t, 2, 0) for t in (q_dec, k_dec, u, w, attn, chunk_decay))
    S0 = jnp.zeros((B, H, d, dv), dt)
    _, o = lax.scan(step, S0, xs)
    return jnp.moveaxis(o, 0, 2).reshape(B, H, T, dv)


def _l2norm(t):
    tf = t.astype(jnp.float32)
    return (tf * lax.rsqrt(jnp.sum(tf * tf, axis=-1, keepdims=True) + EPS)).astype(t.dtype)


def setup_inputs(seed: int = 0) -> dict:
    key = jax.random.key(seed)
    ks = jax.random.split(key, 16)
    f32 = jnp.float32
    n = lambda k, shape: jax.random.normal(k, shape, f32)
    return {
        "x": n(ks[0], (BATCH, SEQ, D_MODEL)),
        "ln1_gain": 1.0 + 0.02 * n(ks[1], (DEPTH, D_MODEL)),
        "w_in": n(ks[2], (DEPTH, D_MODEL, IN_COLS)) * (D_MODEL ** -0.5),
        "ret_norm_gain": 1.0 + 0.02 * n(ks[3], (DEPTH, RET_WIDTH)),
        "gdn_conv_w": n(ks[4], (DEPTH, CONV_WIDTH, GDN_QKV_COLS)) * (CONV_WIDTH ** -0.5),
        "gdn_A_log": jnp.log(jax.random.uniform(ks[5], (DEPTH, GDN_HEADS), f32, 1.0, 16.0)),
        "gdn_dt_bias": 0.1 * n(ks[6], (DEPTH, GDN_HEADS)),
        "gdn_norm_gain": 1.0 + 0.02 * n(ks[7], (DEPTH, GDN_HEAD_DIM)),
        "w_out": n(ks[8], (DEPTH, MIX_WIDTH, D_MODEL)) * (MIX_WIDTH ** -0.5),
        "ln2_gain": 1.0 + 0.02 * n(ks[9], (DEPTH, D_MODEL)),
        "w_up": n(ks[10], (DEPTH, D_MODEL, D_FF)) * (D_MODEL ** -0.5),
        "w_down": n(ks[11], (DEPTH, D_FF, D_MODEL)) * (D_FF ** -0.5),
        "final_gain": 1.0 + 0.02 * n(ks[12], (D_MODEL,)),
    }


def reference(x, ln1_gain, w_in, ret_norm_gain, gdn_conv_w, gdn_A_log, gdn_dt_bias,
              gdn_norm_gain, w_out, ln2_gain, w_up, w_down, final_gain):
    f32 = jnp.float32
    B, T, _ = x.shape
    for l in range(DEPTH):
        h = _rmsnorm(x, ln1_gain[l])
        proj = h @ w_in[l]
        splits = np.cumsum([RET_WIDTH, RET_WIDTH, RET_WIDTH, RET_WIDTH,
                            GDN_QKV_COLS, GDN_WIDTH, GDN_HEADS]).tolist()
        rq, rk, rv, rg, gqkv, gz, gb, ga = jnp.split(proj, splits, axis=-1)

        o_ret = _retention(_heads(rq, RET_HEADS), _heads(rk, RET_HEADS), _heads(rv, RET_HEADS))
        y_ret = _head_groupnorm(o_ret, ret_norm_gain[l]) * jax.nn.silu(rg)

        cqkv = jax.nn.silu(_causal_conv(gqkv, gdn_conv_w[l]))
        cq, ck, cv = jnp.split(cqkv, [GDN_WIDTH, 2 * GDN_WIDTH], axis=-1)
        dq = _l2norm(_heads(cq, GDN_HEADS)) * (GDN_HEAD_DIM ** -0.5)
        dk = _l2norm(_heads(ck, GDN_HEADS))
        dv = _heads(cv, GDN_HEADS)
        beta = jax.nn.sigmoid(gb.astype(f32)).transpose(0, 2, 1)
        g = (-jnp.exp(gdn_A_log[l].astype(f32))
             * jax.nn.softplus(ga.astype(f32) + gdn_dt_bias[l].astype(f32))).transpose(0, 2, 1)
        o_gdn = _gated_delta(dq, dk, dv, g, beta)
        of = o_gdn.astype(f32)
        o_n = of * lax.rsqrt(jnp.mean(of * of, axis=-1, keepdims=True) + EPS) * gdn_norm_gain[l].astype(f32)
        y_gdn = _merge_heads(o_n.astype(x.dtype)) * jax.nn.silu(gz)

        mix = jnp.concatenate([y_ret, y_gdn], axis=-1) @ w_out[l]
        x = x + mix

        h2 = _rmsnorm(x, ln2_gain[l])
        x = x + jnp.square(jax.nn.relu(h2 @ w_up[l])) @ w_down[l]
    return _rmsnorm(x, final_gain)
```

```python
import numpy as np
from contextlib import ExitStack
import concourse.bass as bass
import concourse.mybir as mybir
from concourse.bass_utils import run_bass_kernel_spmd

F32 = mybir.dt.float32
BF16 = mybir.dt.bfloat16
AF = mybir.ActivationFunctionType
ALU = mybir.AluOpType

P = 128
TB = 512
NT = TB // P
EPS = 1e-6
ROPE_BASE = 10000.0
SEM_ROT = 30000


class Sched:
    ENGS = ("pe", "dve", "act", "pool", "sp")

    def __init__(self, nc, es):
        self.nc, self.es = nc, es
        self.streams = {e: [] for e in self.ENGS}
        self.sems = []
        self.cur = {}
        self.cnt = {}
        self.pe_sems = set()
        self.own_sems = {}
        for e in ("pe", "dve", "act", "pool"):
            self._newsem(e)
        self.known = {e: {} for e in self.ENGS}
        self.lastw = {}
        self.readers = {}
        self.pws = {}
        self.dry = False
        self.rings = {}
        for q, n in (("sp", 8), ("pool", 6), ("act", 4)):
            idx = [self._alloc(f"d{q}{i}") for i in range(n)]
            self.rings[q] = {"idx": idx, "tot": [0] * n, "i": 0}
        self.nps = 0
        self.out_tokens = []
        self.last_tok = {}
        self.barrier_tok = None

    def _alloc(self, name):
        self.sems.append(self.es.enter_context(self.nc.semaphore(name)))
        return len(self.sems) - 1

    def _newsem(self, e):
        i = self._alloc(f"s{e}{len(self.sems)}")
        self.own_sems.setdefault(e, set()).add(i)
        self.cur[e] = i
        self.cnt[e] = 0
        if e == "pe":
            self.pe_sems.add(i)

    def _deps(self, reads, writes, pwrites, nobarrier=False):
        deps = []
        if self.barrier_tok and not nobarrier:
            deps.append(self.barrier_tok)
        for k in reads:
            t = self.lastw.get(k)
            if t:
                deps.append(t)
            deps.extend(self.pws.get(k, {}).items())
        for k in list(writes) + list(pwrites):
            t = self.lastw.get(k)
            if t:
                deps.append(t)
            deps.extend(self.readers.get(k, {}).items())
        for k in writes:
            deps.extend(self.pws.get(k, {}).items())
        return deps

    def _commit(self, tok, reads, writes, pwrites):
        s, v = tok
        for k in reads:
            d = self.readers.setdefault(k, {})
            d[s] = max(d.get(s, 0), v)
        for k in writes:
            self.lastw[k] = tok
            self.readers[k] = {}
            self.pws[k] = {}
        for k in pwrites:
            d = self.pws.setdefault(k, {})
            d[s] = max(d.get(s, 0), v)

    def _waits(self, eng, deps):
        need = {}
        kn = self.known[eng]
        for s, v in deps:
            if eng == "pe" and s in self.pe_sems:
                continue
            if kn.get(s, 0) >= v:
                continue
            if need.get(s, 0) < v:
                need[s] = v
        for s, v in need.items():
            kn[s] = v
        return list(need.items())

    def op(self, eng, fns, reads=(), writes=(), pwrites=()):
        if self.dry:
            return
        if callable(fns):
            fns = [fns]
        lk = [k + "_lk" for k in reads if k[:2] in ("ps", "pb")]
        deps = self._deps(reads, writes, pwrites)
        if lk:
            own = self.own_sems[eng]
            deps = deps + [d for d in self._deps((), lk, ()) if d[0] not in own]
            writes = list(writes) + lk
        waits = self._waits(eng, deps)
        if self.cnt[eng] >= SEM_ROT:
            self._newsem(eng)
        self.cnt[eng] += 1
        tok = (self.cur[eng], self.cnt[eng])
        self.last_tok[eng] = tok
        self._commit(tok, reads, writes, pwrites)
        self.streams[eng].append((waits, fns, (tok[0], 1)))

    def dma(self, q, out, in_, reads=(), writes=(), pwrites=(), is_output=False, nobarrier=False):
        if self.dry:
            return
        ring = self.rings[q]
        i = ring["i"]
        ring["i"] = (i + 1) % len(ring["idx"])
        deps = self._deps(reads, writes, pwrites, nobarrier)
        if ring["tot"][i]:
            deps.append((ring["idx"][i], ring["tot"][i]))
        waits = self._waits(q, deps)
        ring["tot"][i] += 16
        tok = (ring["idx"][i], ring["tot"][i])
        self._commit(tok, reads, writes, pwrites)
        self.streams[q].append((waits, [lambda e, o=out, a=in_: e.dma_start(out=o, in_=a)], (tok[0], 16)))
        if is_output:
            self.out_tokens.append(tok)

    def fence(self, keys=None):
        if self.dry:
            return
        deps = [t for t in self.last_tok.values()]
        for ring in self.rings.values():
            for i, tot in zip(ring["idx"], ring["tot"]):
                if tot:
                    deps.append((i, tot))
        ring = self.rings["sp"]
        i = ring["i"]
        ring["i"] = (i + 1) % len(ring["idx"])
        if ring["tot"][i]:
            deps.append((ring["idx"][i], ring["tot"][i]))
        waits = self._waits("sp", deps)
        ring["tot"][i] += 16
        tok = (ring["idx"][i], ring["tot"][i])
        d0, d1 = self.fence_buf
        self.streams["sp"].append((waits, [lambda e: e.dma_start(out=d0, in_=d1)], (tok[0], 16)))
        self.barrier_tok = tok

    def finish(self):
        if self.dry:
            return
        waits = self._waits("sp", self.out_tokens)
        self.streams["sp"].append((waits, None, None))

    def emit(self):
        block = self.es.enter_context(self.nc.Block())
        sems = self.sems

        def mk(stream):
            def body(e):
                for waits, fns, inc in stream:
                    for s, v in waits:
                        e.wait_ge(sems[s], v)
                    if fns is None:
                        continue
                    ins = None
                    for f in fns:
                        ins = f(e)
                    ins.then_inc(sems[inc[0]], inc[1])
            return body

        block.tensor(mk(self.streams["pe"]))
        block.vector(mk(self.streams["dve"]))
        block.scalar(mk(self.streams["act"]))
        block.gpsimd(mk(self.streams["pool"]))
        block.sync(mk(self.streams["sp"]))


def MM(out, lhsT, rhs, start=True, stop=True):
    return lambda e: e.matmul(out, lhsT=lhsT, rhs=rhs, start=start, stop=stop)


def TR(out, in_, ident):
    return lambda e: e.transpose(out, in_, ident)


def ACT(out, in_, func, bias=None, scale=None, accum=None):
    kw = {}
    if bias is not None:
        kw["bias"] = bias
    if scale is not None:
        kw["scale"] = scale
    if accum is not None:
        kw["accum_out"] = accum
    return lambda e: e.activation(out=out, in_=in_, func=func, **kw)


def TT(out, a, b, op):
    return lambda e: e.tensor_tensor(out=out, in0=a, in1=b, op=op)


def TS(out, a, s1, op0, s2=None, op1=None):
    if op1 is None:
        return lambda e: e.tensor_scalar(out=out, in0=a, scalar1=s1, scalar2=None, op0=op0)
    return lambda e: e.tensor_scalar(out=out, in0=a, scalar1=s1, scalar2=s2, op0=op0, op1=op1)


def STT(out, a, s, b, op0, op1):
    return lambda e: e.scalar_tensor_tensor(out=out, in0=a, scalar=s, in1=b, op0=op0, op1=op1)


def CP(out, in_):
    return lambda e: e.tensor_copy(out=out, in_=in_)


def RCP(out, in_):
    return lambda e: e.reciprocal(out=out, in_=in_)


def build_program(D, DFF, NB, debug=False, stop=False):
    KC = D // P
    RW = D // 2
    GW = D // 2
    RH = RW // 256
    GH = GW // 128
    TC = NB * TB
    IN_COLS = 4 * RW + 4 * GW + 2 * GH
    GB = 4 * RW
    NWT = 3
    WEL = KC * 256
    assert (DFF // 256) * 0 == 0

    nc = bass.Bass("TRN2", target_bir_lowering=False)
    din = lambda n, s: nc.dram_tensor(n, s, F32, kind="ExternalInput").ap()
    xp_d = din("xp", [TC, D])
    xm_d = din("xm", [TC, D])
    win_d = din("w_in", [D, IN_COLS])
    wout_d = din("w_out", [D, D])
    wup_d = din("w_up", [D, DFF])
    wdn_d = din("w_down", [DFF, D])
    rope_d = din("rope", [2, P, 2 * TC])
    retc_d = din("retc", [RH, P, 256])
    zeta_d = din("zeta", [P, RH])
    gdnc_d = din("gdnc", [P, 768])
    g1t_d = din("g1t", [P, KC])
    g2t_d = din("g2t", [P, KC])
    cw_d = din("cw", [P, 4 * 3 * GH])
    rgain_d = din("rgain", [RW])
    ggain_d = din("ggain", [128])
    alog_d = din("alog", [GH])
    dtb_d = din("dtb", [GH])
    fgain_d = din("fgain", [D])
    out_d = nc.dram_tensor("out", [TC, D], F32, kind="ExternalOutput").ap()
    dbg_d = nc.dram_tensor("dbg", [D, TC], BF16, kind="ExternalOutput").ap() if debug else None

    es = ExitStack()
    S = Sched(nc, es)
    sb = lambda n, s, d=F32: es.enter_context(nc.sbuf_tensor("sb_" + n, s, d))

    hT_t = sb("hT", [P, KC * TB], BF16)
    hT = hT_t[:].rearrange("p (k t) -> p k t", k=KC)
    FG_OFF = max(D, 2048)
    YR = sb("YR", [P, max(KC * TB // 2, FG_OFF + D)], F32)
    yT = YR[:, 0:KC * TB // 2].bitcast(BF16).rearrange("p (k t) -> p k t", k=KC)
    xsb = [YR[:, i * (D // 2):(i + 1) * (D // 2)].bitcast(BF16) for i in range(2)]
    fgain = YR[:, FG_OFF:FG_OFF + D]
    aT = YR[:, 0:4 * TB // 2].bitcast(BF16).rearrange("p (k t) -> p k t", k=4)
    ARN = max(NT * D, 16384)
    r_relu = [YR[:, 1024 + ii * TB:1024 + (ii + 1) * TB] for ii in range(2)]
    AR = sb("AR", [P, ARN], F32)
    acc = AR[:, 0:NT * D].rearrange("p (t d) -> p t d", t=NT)
    wt_t = [sb(f"wt{i}", [P, WEL], BF16) for i in range(NWT)]
    Rst = sb("Rst", [P, RH * 512], F32)
    Sst = sb("Sst", [P, GH * 128], F32)
    halo = sb("halo", [P, 3 * GH * 3], F32)
    g1t = sb("g1t", [P, KC])
    g2t = sb("g2t", [P, KC])
    cw = sb("cw", [P, 12 * GH])
    zeta = sb("zeta", [P, RH])
    identb = sb("identb", [P, P], BF16)
    small = sb("small", [P, 64])
    S.fence_buf = (small[:, 60:61], small[:, 61:62])
    ss = small[:, 0:8]
    rstd = small[:, 8:16]
    gst = small[:, 16:32]
    negA = sb("negA", [P, GH])
    dtb = sb("dtb", [P, GH])

    class Carver:
        def __init__(self, base, limit):
            self.o, self.limit = base, limit

        def f32(self, n):
            a = AR[:, self.o:self.o + n]
            self.o += n
            assert self.o <= self.limit, (self.o, self.limit)
            return a

        def bf16(self, n):
            assert n % 2 == 0
            a = AR[:, self.o:self.o + n // 2].bitcast(BF16)
            self.o += n // 2
            assert self.o <= self.limit, (self.o, self.limit)
            return a

    cv = Carver(0, ARN)
    gdnc = cv.f32(768)
    Lc, Uc, identf, ones = (gdnc[:, i * 128:(i + 1) * 128] for i in range(4))
    NEG2 = gdnc[:, 512:768]
    ropec = cv.f32(2 * TB).rearrange("p (c t) -> p c t", c=2)
    ggain = cv.f32(128)
    GDN_BASE = cv.o
    rgain = cv.f32(RW)
    CONST_END = cv.o
    xt = [AR[:, CONST_END + i * D:CONST_END + (i + 1) * D] for i in range(2)]
    assert CONST_END + 2 * D <= ARN
    cr = Carver(CONST_END, ARN)
    r_tmp = [cr.f32(TB) for _ in range(4)]
    r_retc = [cr.f32(256) for _ in range(2)]
    r_qr = [cr.bf16(2 * TB).rearrange("p (c t) -> p c t", c=2) for _ in range(2)]
    r_qx = [cr.bf16(2 * TB).rearrange("p (c t) -> p c t", c=2) for _ in range(2)]
    r_kr = [cr.bf16(2 * TB).rearrange("p (c t) -> p c t", c=2) for _ in range(2)]
    r_v = [[cr.bf16(256) for _ in range(NT)] for _ in range(2)]
    r_g1 = [cr.f32(256) for _ in range(NT)]
    r_g = [r_g1, r_g1]
    r_kd = [[cr.bf16(256) for _ in range(NT)] for _ in range(2)]
    r_stm = [cr.bf16(128) for _ in range(2)]
    r_yn = [cr.f32(256) for _ in range(2)]
    r_yb = [cr.bf16(256) for _ in range(2)]
    r_sq = cr.f32(256)
    r_rbf = cr.bf16(512)
    r_st = cr.f32(16)
    RET_KEYS = []
    cg = Carver(GDN_BASE, ARN)
    g_u2 = [cg.f32(TB + 4) for _ in range(2)]
    g_acc2 = [cg.f32(TB) for _ in range(2)]
    g_cs2 = [cg.f32(TB) for _ in range(2)]
    g_u, g_acc, g_cs, g_rs = g_u2[0], g_acc2[0], g_cs2[0], g_u2[1]
    g_q = [cg.f32(TB) for _ in range(2)]
    g_k = [cg.f32(TB) for _ in range(2)]
    g_v = [cg.f32(TB) for _ in range(2)]
    g_gz = [cg.f32(256) for _ in range(NT)]
    g_beta = [cg.f32(GH) for _ in range(NT)]
    g_g = [cg.f32(GH) for _ in range(NT)]
    g_eg = [cg.f32(3 * GH) for _ in range(NT)]
    g_bw = [cg.f32(GH) for _ in range(NT)]
    g_xa = cg.f32(GH)
    g_e1 = cg.f32(GH)
    NBUF = 4
    g_rhsu = [cg.f32(128) for _ in range(NBUF)]
    g_rhsw = [cg.f32(128) for _ in range(NBUF)]
    g_kdec = [cg.f32(128) for _ in range(NBUF)]
    g_G2 = [cg.f32(128) for _ in range(NBUF)]
    g_dec = [cg.f32(256) for _ in range(NBUF)]
    g_dm = g_dec
    g_L = [cg.f32(128) for _ in range(NBUF)]
    g_UP = [cg.f32(256) for _ in range(NBUF)]
    g_attn = [cg.f32(128) for _ in range(NBUF)]
    g_uw = [cg.f32(256) for _ in range(NBUF)]
    g_vnew, g_a1, g_o = g_G2, g_rhsu, g_rhsw
    g_Lb = [a[:, 0:64].bitcast(BF16) for a in g_L]
    g_UPb = [a[:, 0:128].bitcast(BF16) for a in g_UP]
    g_rhsub = [a[:, 0:64].bitcast(BF16) for a in g_rhsu]
    g_rhswb = [a[:, 0:64].bitcast(BF16) for a in g_rhsw]
    g_junk2 = [g_dec[ii][:, 0:128] for ii in range(NBUF)]
    g_yb = [cg.bf16(128) for _ in range(NBUF)]
    g_st = cg.f32(2 * NBUF)

    NPS = 6
    psf = [es.enter_context(nc.psum_tensor(f"psf{i}", [P, 512], F32)) for i in range(NPS)]
    psb = [es.enter_context(nc.psum_tensor(f"psb{i}", [P, 1024], BF16)) for i in range(2)]
    st = {"ps": 0, "pb": 0, "wi": 0, "wrel": 0}

    held = {}

    def PS(hold=False):
        for _ in range(NPS):
            i = st["ps"]
            st["ps"] = (i + 1) % NPS
            if not held.get(f"ps{i}"):
                if hold:
                    held[f"ps{i}"] = True
                return psf[i], f"ps{i}"
        assert not hold, "no free PSUM bank"
        return None, None

    def PB(hold=False):
        for _ in range(2):
            i = st["pb"]
            st["pb"] = 1 - i
            if not held.get(f"pb{i}"):
                if hold:
                    held[f"pb{i}"] = True
                return psb[i], f"pb{i}"
        return None, None

    def PREL(k):
        held[k] = False

    wplan = []

    def WGET(src, kc, cols):
        n = st["wi"]
        st["wi"] = n + 1
        buf = wt_t[n % NWT]
        view = buf[:, 0:kc * cols].rearrange("p (k c) -> p k c", k=kc)
        if S.dry:
            wplan.append((src, kc, cols))
            return view, f"w{n % NWT}"
        assert wplan[n][1:] == (kc, cols)
        assert n < st["wrel"] + NWT, "too many live weight tiles"
        if n == 0:
            for j in range(min(NWT, len(wplan))):
                _wload(j)
        return view, f"w{n % NWT}"

    def WREL():
        m = st["wrel"]
        st["wrel"] = m + 1
        if S.dry:
            return
        if m + NWT < len(wplan):
            _wload(m + NWT)

    def _wload(j):
        src, kc, cols = wplan[j]
        buf = wt_t[j % NWT]
        dst = buf[:, 0:kc * cols].rearrange("p (k c) -> p k c", k=kc)
        S.dma("pool", dst, src.rearrange("(k p) c -> p k c", p=P), writes=[f"w{j % NWT}"], nobarrier=True)

    def rstd_from_ss(ss_ap, n, dst):
        S.op("act", ACT(dst, ss_ap, AF.Sqrt, bias=EPS, scale=1.0 / n), reads=["ssb"], writes=["rsb"])
        S.op("dve", RCP(dst, dst), reads=["rsb"], writes=["rsb"])

    def make_hT(src_tile_fn, src_keys, gT, dst, dst_key, pre_tile=None):
        for t in range(NT):
            if pre_tile is not None:
                pre_tile(t)
            xa = src_tile_fn(t)
            xb = xsb[t % 2]
            xk = f"xsb{t % 2}"
            S.op("act", ACT(xb, xa, AF.Square, accum=ss[:, t:t + 1]), reads=[src_keys[t]], writes=[xk, "ssb"])
            rstd_from_ss(ss[:, t:t + 1], D, rstd[:, t:t + 1])
            S.op("dve", TS(xb, xa, rstd[:, t:t + 1], ALU.mult), reads=[src_keys[t], "rsb"], writes=[xk])
            for k0 in range(0, KC, 4):
                pb, pk = PB()
                S.op("pe", [TR(pb[:, j * 128:(j + 1) * 128], xb[:, (k0 + j) * 128:(k0 + j + 1) * 128], identb[:])
                            for j in range(4)], reads=[xk], writes=[pk])
                eng = "dve" if (k0 // 4) % 2 == 0 else "pool"
                if eng == "pool":
                    for j in range(4):
                        S.op("act", ACT(dst[:, k0 + j, t * 128:(t + 1) * 128], pb[:, j * 128:(j + 1) * 128], AF.Copy,
                                        scale=gT[:, k0 + j:k0 + j + 1]), reads=[pk], pwrites=[dst_key])
                else:
                    S.op("dve", TT(dst[:, k0:k0 + 4, t * 128:(t + 1) * 128],
                                   pb[:, 0:512].rearrange("p (k t) -> p k t", k=4),
                                   gT[:, k0:k0 + 4].unsqueeze(2).to_broadcast([P, 4, 128]), ALU.mult),
                         reads=[pk], pwrites=[dst_key])

    def proj_fm(w, wk, c0, src, src_key, hold=False):
        ps, pk = PS(hold)
        S.op("pe", [MM(ps[:, 0:TB], w[:, kc, c0:c0 + 128], src[:, kc, :], kc == 0, kc == KC - 1) for kc in range(KC)],
             reads=[wk, src_key], writes=[pk])
        return ps, pk

    def proj_tm(w, wk, t, ncols, src, src_key, c0=0):
        ps, pk = PS()
        S.op("pe", [MM(ps[:, 0:ncols], src[:, kc, t * 128:(t + 1) * 128], w[:, kc, c0:c0 + ncols], kc == 0, kc == KC - 1)
                    for kc in range(KC)], reads=[wk, src_key], writes=[pk])
        return ps, pk

    def ret_head(main, h, tok_off):
        b = h % 2
        kq, kx, kk = f"rqr{b}", f"rqx{b}", f"rkr{b}"
        retc = r_retc[b]
        S.dma("sp", retc, retc_d[h], writes=[f"retc{b}"])
        DT = retc[:, 0:128]
        XI = retc[:, 128:256]
        cos = ropec[:, 0, :]
        sin = ropec[:, 1, :]

        def rope(wcol0, dst, dkey, with_xi):
            w, wk = WGET(win_d[:, wcol0:wcol0 + 256], KC, 256)
            p1, k1 = proj_fm(w, wk, 0, hT, "hT")
            p2, k2 = proj_fm(w, wk, 128, hT, "hT")
            WREL()
            t1, t2, t3, t4 = r_tmp
            S.op("dve", TT(t1, p1[:, 0:TB], cos, ALU.mult), reads=[k1, "ropec"], writes=["rt1"])
            S.op("dve", TT(t2, p2[:, 0:TB], sin, ALU.mult), reads=[k2, "ropec"], writes=["rt2"])
            S.op("dve", TT(t3, p1[:, 0:TB], sin, ALU.mult), reads=[k1, "ropec"], writes=["rt3"])
            S.op("dve", TT(t4, p2[:, 0:TB], cos, ALU.mult), reads=[k2, "ropec"], writes=["rt4"])
            if not with_xi:
                S.op("pool", TT(dst[:, 0, :], t1, t2, ALU.subtract), reads=["rt1", "rt2"], pwrites=[dkey])
                S.op("pool", TT(dst[:, 1, :], t3, t4, ALU.add), reads=["rt3", "rt4"], pwrites=[dkey])
                return
            xi_b = XI.unsqueeze(1).to_broadcast([P, NT, 128])
            v3 = lambda a: a.rearrange("p (t c) -> p t c", t=NT)
            S.op("pool", TT(t1, t1, t2, ALU.subtract), reads=["rt1", "rt2"], writes=["rt1"])
            S.op("pool", TT(t3, t3, t4, ALU.add), reads=["rt3", "rt4"], writes=["rt3"])
            S.op("act", ACT(dst[:, 0, :], t1, AF.Copy), reads=["rt1"], pwrites=[dkey])
            S.op("act", ACT(dst[:, 1, :], t3, AF.Copy), reads=["rt3"], pwrites=[dkey])
            S.op("pool", TT(v3(r_qx[b][:, 0, :]), v3(t1), xi_b, ALU.mult), reads=["rt1", f"retc{b}"], pwrites=[kx])
            S.op("pool", TT(v3(r_qx[b][:, 1, :]), v3(t3), xi_b, ALU.mult), reads=["rt3", f"retc{b}"], pwrites=[kx])

        if main:
            rope(h * 256, r_qr[b], kq, True)
        rope(RW + h * 256, r_kr[b], kk, False)
        wv, wvk = WGET(win_d[:, 2 * RW + h * 256:2 * RW + (h + 1) * 256], KC, 256)
        for t in range(NT):
            ps, pk = proj_tm(wv, wvk, t, 256, hT, "hT")
            S.op("act", ACT(r_v[b][t], ps[:, 0:256], AF.Copy), reads=[pk], writes=[f"rv{b}{t}"])
        WREL()
        if main:
            wg, wgk = WGET(win_d[:, 3 * RW + h * 256:3 * RW + (h + 1) * 256], KC, 256)
            for t in range(NT):
                ps, pk = proj_tm(wg, wgk, t, 256, hT, "hT")
                S.op("act", ACT(r_g[b][t], ps[:, 0:256], AF.Silu), reads=[pk], writes=[f"rg{t}"])
                S.op("pool", TT(r_g[b][t], r_g[b][t], rgain[:, h * 256:(h + 1) * 256], ALU.mult),
                     reads=[f"rg{t}", "rgain"], writes=[f"rg{t}"])
            WREL()
        for t in range(NT):
            pb, pk = PB()
            S.op("pe", [TR(pb[:, dc * 128:(dc + 1) * 128], r_kr[b][:, dc, t * 128:(t + 1) * 128], identb[:]) for dc in range(2)],
                 reads=[kk], writes=[pk])
            S.op("act", ACT(r_kd[b][t], pb[:, 0:256], AF.Copy, scale=zeta[:, h:h + 1]), reads=[pk], writes=[f"rkd{b}{t}"])
        R = Rst[:, h * 512:(h + 1) * 512]
        Rk = f"R{h}"
        g128 = float(np.exp(np.log1p(-2.0 ** (-5.0 - h)) * 128.0))
        S.op("act", ACT(r_rbf, R, AF.Copy), reads=[Rk], writes=["rbf"])
        for t in range(NT):
            tsl = slice(t * 128, (t + 1) * 128)
            if main:
                ps_s, ks = PS()
                S.op("pe", [MM(ps_s[:, 0:128], r_kr[b][:, dc, tsl], r_qr[b][:, dc, tsl], dc == 0, dc == 1) for dc in range(2)],
                     reads=[kk, kq], writes=[ks])
                stm = r_stm[t % 2]
                S.op("dve", TT(stm, ps_s[:, 0:128], DT, ALU.mult), reads=[ks, f"retc{b}"], writes=[f"stm{t % 2}"])
                ps_o, ko = PS()
                S.op("pe", [MM(ps_o[:, 0:256], stm, r_v[b][t], True, False),
                            MM(ps_o[:, 0:256], r_qx[b][:, 0, tsl], r_rbf[:, 0:256], False, False),
                            MM(ps_o[:, 0:256], r_qx[b][:, 1, tsl], r_rbf[:, 256:512], False, True)],
                     reads=[f"stm{t % 2}", f"rv{b}{t}", kx, "rbf"], writes=[ko])
            ps_r, kr_ = PS()
            S.op("pe", [MM(ps_r[:, dc * 256:(dc + 1) * 256], r_kd[b][t][:, dc * 128:(dc + 1) * 128], r_v[b][t], True, True)
                        for dc in range(2)], reads=[f"rkd{b}{t}", f"rv{b}{t}"], writes=[kr_])
            S.op("dve", STT(R, R, g128, ps_r[:, 0:512], ALU.mult, ALU.add), reads=[kr_, Rk], writes=[Rk])
            S.op("act", ACT(r_rbf, R, AF.Copy), reads=[Rk], writes=["rbf"])
            if main:
                yn = r_yn[t % 2]
                ynk = f"ryn{t % 2}"
                S.op("act", ACT(yn, ps_o[:, 0:256], AF.Identity, accum=r_st[:, 0:1]), reads=[ko], writes=[ynk, "rst0"])
                S.op("act", ACT(r_sq, ps_o[:, 0:256], AF.Square, accum=r_st[:, 1:2]), reads=[ko], writes=["rsq", "rst1"])
                S.op("dve", TS(r_st[:, 2:3], r_st[:, 0:1], 1.0 / 256, ALU.mult), reads=["rst0"], writes=["rst2"])
                S.op("dve", TT(r_st[:, 3:4], r_st[:, 2:3], r_st[:, 2:3], ALU.mult), reads=["rst2"], writes=["rst3"])
                S.op("dve", STT(r_st[:, 4:5], r_st[:, 1:2], 1.0 / 256, r_st[:, 3:4], ALU.mult, ALU.subtract),
                     reads=["rst1", "rst3"], writes=["rst4"])
                S.op("act", ACT(r_st[:, 4:5], r_st[:, 4:5], AF.Sqrt, bias=EPS), reads=["rst4"], writes=["rst4"])
                S.op("dve", RCP(r_st[:, 4:5], r_st[:, 4:5]), reads=["rst4"], writes=["rst4"])
                S.op("dve", TS(yn, yn, r_st[:, 2:3], ALU.subtract, r_st[:, 4:5], ALU.mult), reads=[ynk, "rst2", "rst4"], writes=[ynk])
                yb = r_yb[t % 2]
                S.op("pool", TT(yb, yn, r_g[b][t], ALU.mult), reads=[ynk, f"rg{t}"], writes=[f"ryb{t % 2}"])
                pb, pk = PB()
                S.op("pe", [TR(pb[:, dc * 128:(dc + 1) * 128], yb[:, dc * 128:(dc + 1) * 128], identb[:]) for dc in range(2)],
                     reads=[f"ryb{t % 2}"], writes=[pk])
                S.op("act", ACT(yT[:, 2 * h:2 * h + 2, tsl], pb[:, 0:256].rearrange("p (k t) -> p k t", k=2), AF.Copy),
                     reads=[pk], pwrites=["yT"])

    def gdn_gates(tok_off):
        w, wk = WGET(win_d[:, GB + 4 * GW:GB + 4 * GW + 2 * GH], KC, 2 * GH)
        for t in range(NT):
            ps, pk = proj_tm(w, wk, t, 2 * GH, hT, "hT")
            S.op("act", ACT(g_beta[t], ps[:, 0:GH], AF.Sigmoid), reads=[pk], writes=[f"gbeta{t}"])
            S.op("dve", TT(g_xa, ps[:, GH:2 * GH], dtb[:], ALU.add), reads=[pk], writes=["gxa"])
            S.op("act", ACT(g_e1, g_xa, AF.Abs), reads=["gxa"], writes=["ge1"])
            S.op("act", ACT(g_e1, g_e1, AF.Exp, scale=-1.0), reads=["ge1"], writes=["ge1"])
            S.op("act", ACT(g_e1, g_e1, AF.Ln, bias=1.0), reads=["ge1"], writes=["ge1"])
            S.op("dve", STT(g_xa, g_xa, 0.0, g_e1, ALU.max, ALU.add), reads=["gxa", "ge1"], writes=["gxa"])
            S.op("dve", TT(g_g[t], g_xa, negA[:], ALU.mult), reads=["gxa"], writes=[f"gg{t}"])
            ps2, pk2 = PS()
            S.op("pe", [MM(ps2[:, 0:GH], Lc, g_g[t]), MM(ps2[:, GH:2 * GH], Uc, g_g[t]), MM(ps2[:, 2 * GH:3 * GH], ones, g_g[t])],
                 reads=[f"gg{t}", "gdnc"], writes=[pk2])
            S.op("act", ACT(g_eg[t], ps2[:, 0:3 * GH], AF.Exp), reads=[pk2], writes=[f"geg{t}"])
            S.op("dve", TT(g_bw[t], g_beta[t], g_eg[t][:, 0:GH], ALU.mult), reads=[f"gbeta{t}", f"geg{t}"], writes=[f"gbw{t}"])
        WREL()

    def gdn_feat(w, wk, j, chunk, dst, dkey, mode, fi):
        gu, gacc, gcs = g_u2[fi], g_acc2[fi], g_cs2[fi]
        ku, ka, kc_ = f"gu{fi}", f"gacc{fi}", f"gcs{fi}"
        ps, pk = proj_fm(w, wk, j * 128, hT, "hT", hold=True)
        hl = halo[:, chunk * 3:(chunk + 1) * 3]
        hk = f"halo{chunk}"
        yield
        if mode == "halo":
            S.op("act", ACT(hl, ps[:, TB - 3:TB], AF.Copy), reads=[pk], writes=[hk])
            PREL(pk)
            return
        S.op("act", ACT(gu[:, 3:3 + TB], ps[:, 0:TB], AF.Copy), reads=[pk], writes=[ku])
        PREL(pk)
        S.op("act", ACT(gu[:, 0:3], hl, AF.Copy), reads=[hk], pwrites=[ku])
        yield
        S.op("act", ACT(hl, gu[:, TB:TB + 3], AF.Copy), reads=[ku], writes=[hk])
        cwc = lambda jj: cw[:, jj * 3 * GH + chunk:jj * 3 * GH + chunk + 1]
        S.op("act", ACT(gacc, gu[:, 3:3 + TB], AF.Copy, scale=cwc(3)), reads=[ku], writes=[ka])
        yield
        for jj in (2, 1, 0):
            S.op("dve", STT(gacc, gu[:, jj:jj + TB], cwc(jj), gacc, ALU.mult, ALU.add), reads=[ku, ka], writes=[ka])
            yield
        if mode == "v":
            S.op("act", ACT(dst, gacc, AF.Silu), reads=[ka], writes=[dkey])
            return
        S.op("act", ACT(gcs, gacc, AF.Silu), reads=[ka], writes=[kc_])
        yield
        gsq = gacc
        S.op("pool", TT(gsq, gcs, gcs, ALU.mult), reads=[kc_], writes=[ka])
        yield
        ps2, pk2 = PS(True)
        S.op("pe", [MM(ps2[:, hh * 256:(hh + 1) * 256], ones, gsq[:, hh * 256:(hh + 1) * 256]) for hh in range(TB // 256)],
             reads=[ka, "gdnc"], writes=[pk2])
        yield
        grs = gu[:, 0:TB]
        S.op("act", ACT(grs, ps2[:, 0:TB], AF.Sqrt, bias=EPS), reads=[pk2], writes=[ku])
        PREL(pk2)
        yield
        S.op("dve", RCP(grs, grs), reads=[ku], writes=[ku])
        yield
        if mode == "q":
            S.op("dve", STT(dst, gcs, float(128 ** -0.5), grs, ALU.mult, ALU.mult), reads=[kc_, ku], writes=[dkey])
        else:
            S.op("pool", TT(dst, gcs, grs, ALU.mult), reads=[kc_, ku], writes=[dkey])

    cnt = {"ch": 0}

    def gdn_chunk(main, hd, t, b, i):
        tsl = slice(t * 128, (t + 1) * 128)
        hs = slice(hd, hd + 1)
        kT, qT, vT = g_k[b], g_q[b], g_v[b]
        K = lambda n: f"{n}{i}"
        ps, pk = PS(True)
        S.op("pe", [TR(ps[:, 0:128], kT[:, tsl], identf), TR(ps[:, 128:256], vT[:, tsl], identf)],
             reads=[f"gk{b}", f"gv{b}", "gdnc"], writes=[pk])
        S.op("act", ACT(g_G2[i], Uc, AF.Copy, scale=g_g[t][:, hs]), reads=["gdnc", f"gg{t}"], writes=[K("G2")])
        yield
        S.op("act", ACT(g_rhswb[i], ps[:, 0:128], AF.Copy, scale=g_bw[t][:, hs]), reads=[pk, f"gbw{t}"], writes=[K("rhsw")])
        S.op("act", ACT(g_kdec[i], ps[:, 0:128], AF.Copy, scale=g_eg[t][:, GH + hd:GH + hd + 1]), reads=[pk, f"geg{t}"], writes=[K("kdec")])
        S.op("act", ACT(g_rhsub[i], ps[:, 128:256], AF.Copy, scale=g_beta[t][:, hs]), reads=[pk, f"gbeta{t}"], writes=[K("rhsu")])
        PREL(pk)
        psD, kD = PS(True)
        mms = [MM(psD[:, 0:128], Lc, g_G2[i]), MM(psD[:, 128:256], g_G2[i], Lc), MM(psD[:, 256:384], kT[:, tsl], kT[:, tsl])]
        if main:
            mms.append(MM(psD[:, 384:512], kT[:, tsl], qT[:, tsl]))
        S.op("pe", mms, reads=[K("G2"), "gdnc", f"gk{b}"] + ([f"gq{b}"] if main else []), writes=[kD])
        yield
        S.op("dve", STT(g_dm[i], psD[:, 0:256], 0.0, NEG2, ALU.min, ALU.add), reads=[kD, "gdnc"], writes=[K("dec")])
        yield
        S.op("act", ACT(g_dec[i], g_dm[i], AF.Exp), reads=[K("dec")], writes=[K("dec")])
        yield
        L, UP = g_Lb[i], g_UPb[i]
        S.op("dve", STT(L, psD[:, 256:384], g_beta[t][:, hs], g_dec[i][:, 0:128], ALU.mult, ALU.mult),
             reads=[kD, K("dec"), f"gbeta{t}"], writes=[K("L")])
        if main:
            S.op("dve", TT(g_attn[i], psD[:, 384:512], g_dec[i][:, 128:256], ALU.mult), reads=[kD, K("dec")], writes=[K("attn")])
        PREL(kD)
        yield
        psU, kU = PB(True)
        while psU is None:
            yield
            psU, kU = PB(True)
        S.op("pe", TR(psU[:, 0:128], L, identb[:]), reads=[K("L")], writes=[kU])
        yield
        S.op("act", ACT(UP[:, 0:128], psU[:, 0:128], AF.Copy), reads=[kU], pwrites=[K("UP")])
        S.op("dve", STT(UP[:, 128:256], psU[:, 0:128], -1.0, identf, ALU.mult, ALU.add), reads=[kU, "gdnc"], pwrites=[K("UP")])
        PREL(kU)
        yield
        Pm = UP[:, 128:256]
        for s_ in range(6):
            psA, kA = PS(True)
            S.op("pe", [MM(psA[:, 0:256], L, UP), MM(psA[:, 256:384], UP[:, 0:128], L)], reads=[K("L"), K("UP")], writes=[kA])
            yield
            S.op("act", ACT(UP[:, 0:128], psA[:, 0:128], AF.Copy), reads=[kA], pwrites=[K("UP")])
            S.op("act", ACT(L, psA[:, 256:384], AF.Copy), reads=[kA], writes=[K("L")])
            if s_ > 0:
                S.op("dve", TT(Pm, psA[:, 128:256], Pm, ALU.add), reads=[kA], pwrites=[K("UP")])
            PREL(kA)
            yield
        psA, kA = PS(True)
        S.op("pe", MM(psA[:, 0:128], L, Pm), reads=[K("L"), K("UP")], writes=[kA])
        yield
        S.op("dve", TT(Pm, psA[:, 0:128], Pm, ALU.add), reads=[kA, K("UP")], writes=[K("UP")])
        PREL(kA)
        yield
        psu, ku = PS(True)
        S.op("pe", [MM(psu[:, 0:128], Pm, g_rhsub[i]), MM(psu[:, 128:256], g_rhswb[i], Pm)],
             reads=[K("UP"), K("rhsu"), K("rhsw")], writes=[ku])
        yield
        S.op("act", ACT(g_uw[i], psu[:, 0:256], AF.Copy), reads=[ku], writes=[K("uw")])
        PREL(ku)
        yield
        while t > 0 and not rec_done.get((hd, t - 1)):
            yield
        Sm = Sst[:, hd * 128:(hd + 1) * 128]
        Sk = f"S{hd}"
        psS, kS = PS(True)
        mms = [MM(psS[:, 0:128], g_uw[i][:, 128:256], Sm)]
        if main:
            mms.append(MM(psS[:, 128:256], qT[:, tsl], Sm))
        S.op("pe", mms, reads=[K("uw"), Sk] + ([f"gq{b}"] if main else []), writes=[kS])
        yield
        S.op("dve", STT(g_vnew[i], psS[:, 0:128], -1.0, g_uw[i][:, 0:128], ALU.mult, ALU.add), reads=[K("uw"), kS], writes=[K("G2")])
        if main:
            S.op("act", ACT(g_a1[i], psS[:, 128:256], AF.Copy, scale=g_eg[t][:, hs]), reads=[kS, f"geg{t}"], writes=[K("rhsu")])
        PREL(kS)
        yield
        psO, kO = PS(True)
        mms = [MM(psO[:, 128:256], g_kdec[i], g_vnew[i])]
        if main:
            mms.append(MM(psO[:, 0:128], g_attn[i], g_vnew[i]))
        S.op("pe", mms, reads=[K("kdec"), K("G2")] + ([K("attn")] if main else []), writes=[kO])
        yield
        S.op("dve", STT(Sm, Sm, g_eg[t][:, 2 * GH + hd:2 * GH + hd + 1], psO[:, 128:256], ALU.mult, ALU.add),
             reads=[kO, Sk, f"geg{t}"], writes=[Sk])
        rec_done[(hd, t)] = True
        if not main:
            PREL(kO)
        if main:
            S.op("dve", TT(g_o[i], psO[:, 0:128], g_a1[i], ALU.add), reads=[kO, K("rhsu")], writes=[K("rhsw")])
            PREL(kO)
            yield
            S.op("act", ACT(g_junk2[i], g_o[i], AF.Square, accum=g_st[:, 2 * i:2 * i + 1]), reads=[K("rhsw")], writes=[K("dec"), K("gst0")])
            yield
            r1 = g_st[:, 2 * i + 1:2 * i + 2]
            S.op("act", ACT(r1, g_st[:, 2 * i:2 * i + 1], AF.Sqrt, bias=EPS, scale=1.0 / 128), reads=[K("gst0")], writes=[K("gst1")])
            yield
            S.op("dve", RCP(r1, r1), reads=[K("gst1")], writes=[K("gst1")])
            jj = hd % 2
            S.op("dve", STT(g_yb[i], g_o[i], r1, g_gz[t][:, jj * 128:(jj + 1) * 128], ALU.mult, ALU.mult),
                 reads=[K("rhsw"), K("gst1"), f"ggz{t}"], writes=[K("yb")])
            yield
            pb, pk = PB(True)
            while pb is None:
                yield
                pb, pk = PB(True)
            S.op("pe", TR(pb[:, 0:128], g_yb[i], identb[:]), reads=[K("yb")], writes=[pk])
            yield
            S.op("act", ACT(yT[:, RW // 128 + hd, tsl], pb[:, 0:128], AF.Copy), reads=[pk], pwrites=["yT"])
            PREL(pk)

    rec_done = {}

    def lockstep(gens):
        gens = list(gens)
        while gens:
            for g in list(gens):
                try:
                    next(g)
                except StopIteration:
                    gens.remove(g)

    def gdn_pair(main, pair, last_prefix):
        c0 = pair * 256
        if main:
            wz, wzk = WGET(win_d[:, GB + 3 * GW + c0:GB + 3 * GW + c0 + 256], KC, 256)
            for t in range(NT):
                ps, pk = proj_tm(wz, wzk, t, 256, hT, "hT")
                S.op("act", ACT(g_gz[t], ps[:, 0:256], AF.Silu), reads=[pk], writes=[f"ggz{t}"])
                S.op("pool", TT(g_gz[t].rearrange("p (j c) -> p j c", j=2), g_gz[t].rearrange("p (j c) -> p j c", j=2),
                                ggain.unsqueeze(1).to_broadcast([P, 2, 128]), ALU.mult), reads=[f"ggz{t}", "ggain"], writes=[f"ggz{t}"])
            WREL()
        if main or last_prefix:
            wq, wqk = WGET(win_d[:, GB + c0:GB + c0 + 256], KC, 256)
            lockstep([gdn_feat(wq, wqk, j, pair * 2 + j, g_q[j] if main else None, f"gq{j}", "q" if main else "halo", j)
                      for j in range(2)])
            WREL()
        wk_, wkk = WGET(win_d[:, GB + GW + c0:GB + GW + c0 + 256], KC, 256)
        lockstep([gdn_feat(wk_, wkk, j, GH + pair * 2 + j, g_k[j], f"gk{j}", "k", j) for j in range(2)])
        WREL()
        wv, wvk = WGET(win_d[:, GB + 2 * GW + c0:GB + 2 * GW + c0 + 256], KC, 256)
        lockstep([gdn_feat(wv, wvk, j, 2 * GH + pair * 2 + j, g_v[j], f"gv{j}", "v", j) for j in range(2)])
        WREL()
        rec_done.clear()
        for t0 in range(0, NT, 2):
            lockstep([gdn_chunk(main, pair * 2 + j, t, j, 2 * (t % 2) + j) for t in (t0, t0 + 1) for j in range(2)])

    ALLW = [f"w{i}" for i in range(NWT)]

    def load_block_consts(tok_off_rope):
        S.dma("sp", gdnc, gdnc_d, writes=["gdnc"])
        S.dma("sp", ropec, rope_d[:, :, tok_off_rope:tok_off_rope + TB].rearrange("c p t -> p c t"), writes=["ropec"])
        S.dma("sp", rgain, rgain_d.partition_broadcast(P), writes=["rgain"])
        S.dma("sp", ggain, ggain_d.partition_broadcast(P), writes=["ggain"])

    SCR = ["AR"]

    def block(main, bi):
        x_d = xm_d if main else xp_d
        r0 = bi * TB
        at = lambda stage: isinstance(stop, tuple) and stop == (main, bi, stage)
        S.fence(["AR", "acc0", "acc1", "acc2", "acc3", "xt0", "xt1"])
        load_block_consts((TC if main else 0) + r0)
        def load_x(t):
            S.dma("sp", xt[t % 2], x_d[r0 + t * 128:r0 + (t + 1) * 128, :], writes=[f"xt{t % 2}"])

        def pre_tile(t):
            if t == 0:
                load_x(0)
            if t + 1 < NT:
                load_x(t + 1)

        make_hT(lambda t: xt[t % 2], [f"xt{t % 2}" for t in range(NT)], g1t, hT, "hT", pre_tile)
        if at("hT"):
            return "STOP"
        S.fence(["xt0", "xt1", "AR"])
        for h in range(RH):
            ret_head(main, h, r0)
        if at("ret"):
            return "STOP"
        S.fence(["AR", "rt1", "rt2", "rt3", "rt4"])
        gdn_gates(r0)
        if at("gates"):
            return "STOP"
        last_prefix = (not main) and bi == NB - 1
        for pair in range(GH // 2):
            if gdn_pair(main, pair, last_prefix) == "STOP":
                return "STOP"
        if at("gdn"):
            return "STOP"
        if not main:
            return
        if stop == True or at("gdn"):
            return "STOP"
        S.fence(["AR"] + [f"acc{t}" for t in range(NT)])
        for t in range(NT):
            S.dma("sp", acc[:, t, :], x_d[r0 + t * 128:r0 + (t + 1) * 128, :], writes=[f"acc{t}"])
        if debug:
            S.dma("sp", dbg_d.rearrange("(k p) t -> p k t", p=P)[:, :, r0:r0 + TB], yT, reads=["yT"], writes=["dbgout"], is_output=True)
        for cgi in range(D // 256):
            w, wk = WGET(wout_d[:, cgi * 256:(cgi + 1) * 256], KC, 256)
            for t in range(NT):
                ps, pk = proj_tm(w, wk, t, 256, yT, "yT")
                sl = acc[:, t, cgi * 256:(cgi + 1) * 256]
                S.op("dve", TT(sl, ps[:, 0:256], sl, ALU.add), reads=[pk], writes=[f"acc{t}"])
            WREL()
        if at("outproj"):
            return "STOP"
        S.fence(["yT", "xsb0", "xsb1", "hT", "dbgout"])
        make_hT(lambda t: acc[:, t, :], [f"acc{t}" for t in range(NT)], g2t, hT, "hT")
        S.fence(["xsb0", "xsb1", "aT", "hT"])
        S.dma("sp", fgain, fgain_d.partition_broadcast(P), writes=["fgain"])
        for F in range(DFF // 512):
            for half in range(2):
                wu, wuk = WGET(wup_d[:, F * 512 + half * 256:F * 512 + (half + 1) * 256], KC, 256)
                for dc in range(2):
                    ps, pk = proj_fm(wu, wuk, dc * 128, hT, "hT")
                    fc = half * 2 + dc
                    S.op("act", ACT(r_relu[fc % 2], ps[:, 0:TB], AF.Relu), reads=[pk], writes=[f"relu{fc % 2}"])
                    S.op("act", ACT(aT[:, fc, :], r_relu[fc % 2], AF.Square), reads=[f"relu{fc % 2}"], writes=[f"aT{fc}"])
                WREL()
            wd = []
            for half in range(2):
                w, wk = WGET(wdn_d[F * 512 + half * 256:F * 512 + (half + 1) * 256, :], 2, D)
                wd.append((w, wk))
            for t in range(NT):
                for cgi in range(D // 512):
                    ps, pk = PS()
                    S.op("pe", [MM(ps[:, 0:512], aT[:, fc, t * 128:(t + 1) * 128], wd[fc // 2][0][:, fc % 2, cgi * 512:(cgi + 1) * 512],
                                   fc == 0, fc == 3) for fc in range(4)],
                         reads=[f"aT{fc}" for fc in range(4)] + [wd[0][1], wd[1][1]], writes=[pk])
                    sl = acc[:, t, cgi * 512:(cgi + 1) * 512]
                    S.op("dve", TT(sl, ps[:, 0:512], sl, ALU.add), reads=[pk], writes=[f"acc{t}"])
            WREL()
            WREL()
        if at("mlp"):
            return "STOP"
        S.fence()
        for t in range(NT):
            a = acc[:, t, :]
            S.op("act", ACT(xsb[0], a, AF.Square, accum=ss[:, t:t + 1]), reads=[f"acc{t}"], writes=["xsb0", "ssb"])
            rstd_from_ss(ss[:, t:t + 1], D, rstd[:, t:t + 1])
            S.op("dve", STT(a, a, rstd[:, t:t + 1], fgain, ALU.mult, ALU.mult), reads=["rsb", "fgain"], writes=[f"acc{t}"])
            S.dma("sp", out_d[r0 + t * 128:r0 + (t + 1) * 128, :], a, reads=[f"acc{t}"], writes=[f"out{t}"], is_output=True)

    def program():
        st["ps"] = 0
        st["pb"] = 0
        st["wi"] = 0
        st["wrel"] = 0
        cnt["ch"] = 0
        held.clear()
        S.op("pool", lambda e: e.memset(small[:], 0.0), writes=["ssb", "rsb"])
        S.dma("sp", g1t[:], g1t_d, writes=["g1t"])
        S.dma("sp", g2t[:], g2t_d, writes=["g2t"])
        S.dma("sp", cw[:], cw_d, writes=["cw"])
        S.dma("sp", zeta[:], zeta_d, writes=["zeta"])
        S.dma("sp", negA[:], alog_d.partition_broadcast(P), writes=["negA"])
        S.dma("sp", dtb[:], dtb_d.partition_broadcast(P), writes=["dtb"])
        S.dma("sp", AR[:, 0:128], gdnc_d[:, 256:384], writes=["AR"])
        S.op("dve", CP(identb[:], AR[:, 0:128]), reads=["AR"], writes=["identb"])
        S.op("act", ACT(negA[:], negA[:], AF.Exp), reads=["negA"], writes=["negA"])
        S.op("dve", TS(negA[:], negA[:], -1.0, ALU.mult), reads=["negA"], writes=["negA"])
        S.op("pool", lambda e: e.memset(Rst[:], 0.0), writes=[f"R{h}" for h in range(RH)])
        S.op("pool", lambda e: e.memset(Sst[:], 0.0), writes=[f"S{h}" for h in range(GH)])
        S.op("pool", lambda e: e.memset(halo[:], 0.0), writes=[f"halo{c}" for c in range(3 * GH)])
        S.fence(["g1t", "g2t", "cw", "zeta", "negA", "dtb", "identb", "AR"])
        stopped = False
        for bi in range(NB):
            if block(False, bi) == "STOP":
                stopped = True
                break
        for bi in range(NB):
            if stopped or block(True, bi) == "STOP":
                break
        S.finish()

    S.dry = True
    program()
    S.dry = False
    program()
    S.emit()
    es.close()
    nc._dbg_layout = {"g_q": g_q, "g_k": g_k, "g_v": g_v, "g_beta": g_beta, "g_g": g_g, "g_eg": g_eg, "g_bw": g_bw,
                      "g_rhsu": g_rhsu, "g_rhsw": g_rhsw, "g_kdec": g_kdec, "g_G2": g_G2, "g_dm": g_dm, "g_dec": g_dec,
                      "g_attn": g_attn, "g_uw": g_uw,
                      "g_gz": g_gz}
    return nc


def _const_tables(D, TC, j):
    RW = D // 2
    RH = RW // 256
    GH = (D // 2) // 128
    f32 = np.float32
    inv = (f32(ROPE_BASE) ** (-(np.arange(128, dtype=f32)) * f32(2.0 / 256))).astype(f32)
    pos_main = (j * TC + np.arange(TC)).astype(f32)
    pos_pre = (max(j - 1, 0) * TC + np.arange(TC)).astype(f32)
    pos = np.concatenate([pos_pre, pos_main])
    ang = (pos[None, :] * inv[:, None]).astype(f32)
    rope = np.stack([np.cos(ang), np.sin(ang)]).astype(f32)
    idx = np.arange(128)
    c = idx[None, :]
    m = idx[:, None]
    retc = np.zeros((RH, 128, 256), f32)
    zeta = np.zeros((128, RH), f32)
    for h in range(RH):
        lg = np.log1p(-2.0 ** (-5.0 - h))
        same = (c // 64) == (m // 64)
        dt = np.where(same, np.exp(lg * np.abs(c - m)), np.where(c > m, np.exp(lg * (c - m)), 0.0))
        retc[h, :, 0:128] = dt / 16.0
        retc[h, :, 128:256] = np.exp(lg * (idx + 1.0))[None, :]
        zeta[:, h] = np.exp(lg * (127.0 - idx)) / 16.0
    jj = idx[:, None]
    cc = idx[None, :]
    Lc = (jj <= cc).astype(f32)
    Uc = (jj > cc).astype(f32)
    ident = np.eye(128, dtype=f32)
    ones = np.ones((128, 128), f32)
    NEGS = np.where(jj > cc, 0.0, -1e4).astype(f32)
    NEGT = np.where(cc >= jj, 0.0, -1e4).astype(f32)
    gdnc = np.concatenate([Lc, Uc, ident, ones, NEGS, NEGT], axis=1).astype(f32)
    return rope, retc, zeta, gdnc


def _run(x, ln1_gain, w_in, ret_norm_gain, gdn_conv_w, gdn_A_log, gdn_dt_bias, gdn_norm_gain,
         w_out, ln2_gain, w_up, w_down, final_gain, debug=False, stop=False):
    x = np.asarray(x, np.float32)
    B, T, D = x.shape
    DFF = w_up.shape[-1]
    TC = T // 2
    NB = TC // TB
    KC = D // 128
    GH = (D // 2) // 128
    nc = build_program(D, DFF, NB, debug=debug, stop=stop)
    _run.nc = nc
    f = lambda a: np.ascontiguousarray(np.asarray(a, np.float32))
    w_in0, w_out0, w_up0, w_down0 = f(w_in[0]), f(w_out[0]), f(w_up[0]), f(w_down[0])
    g1t = f(np.asarray(ln1_gain[0]).reshape(KC, 128).T)
    g2t = f(np.asarray(ln2_gain[0]).reshape(KC, 128).T)
    cwa = np.asarray(gdn_conv_w[0], np.float32)
    cw = f(cwa.reshape(4, 3 * GH, 128).transpose(2, 0, 1).reshape(128, 12 * GH))
    in_maps = []
    zeros = np.zeros((TC, D), np.float32)
    for c in range(2 * B):
        b, j = divmod(c, 2)
        rope, retc, zeta, gdnc = _const_tables(D, TC, j)
        in_maps.append({
            "xp": f(x[b, 0:TC]) if j == 1 else zeros,
            "xm": f(x[b, j * TC:(j + 1) * TC]),
            "w_in": w_in0, "w_out": w_out0, "w_up": w_up0, "w_down": w_down0,
            "rope": rope, "retc": retc, "zeta": zeta, "gdnc": gdnc,
            "g1t": g1t, "g2t": g2t, "cw": cw,
            "rgain": f(ret_norm_gain[0]), "ggain": f(gdn_norm_gain[0]),
            "alog": f(gdn_A_log[0]), "dtb": f(gdn_dt_bias[0]), "fgain": f(final_gain),
        })
    res = run_bass_kernel_spmd(nc, in_maps, core_ids=list(range(2 * B)))
    out = np.empty((B, T, D), np.float32)
    for c in range(2 * B):
        b, j = divmod(c, 2)
        out[b, j * TC:(j + 1) * TC] = np.asarray(res.results[c]["out"], np.float32)
    if debug:
        return out, [np.asarray(r["dbg"]) for r in res.results]
    return out


def kernel(x, ln1_gain, w_in, ret_norm_gain, gdn_conv_w, gdn_A_log, gdn_dt_bias, gdn_norm_gain,
           w_out, ln2_gain, w_up, w_down, final_gain):
    return _run(x, ln1_gain, w_in, ret_norm_gain, gdn_conv_w, gdn_A_log, gdn_dt_bias, gdn_norm_gain,
                w_out, ln2_gain, w_up, w_down, final_gain)
```

```python
import numpy as np
from contextlib import ExitStack
import concourse.bass as bass
import concourse.mybir as mybir
from concourse.bass_utils import run_bass_kernel_spmd

F32 = mybir.dt.float32
BF16 = mybir.dt.bfloat16
AF = mybir.ActivationFunctionType
ALU = mybir.AluOpType

P = 128
TB = 512
NT = TB // P
EPS = 1e-6
ROPE_BASE = 10000.0
SEM_ROT = 30000


class Sched:
    ENGS = ("pe", "dve", "act", "pool", "sp")

    def __init__(self, nc, es):
        self.nc, self.es = nc, es
        self.streams = {e: [] for e in self.ENGS}
        self.sems = []
        self.cur = {}
        self.cnt = {}
        self.pe_sems = set()
        self.own_sems = {}
        for e in ("pe", "dve", "act", "pool"):
            self._newsem(e)
        self.known = {e: {} for e in self.ENGS}
        self.lastw = {}
        self.readers = {}
        self.pws = {}
        self.dry = False
        self.rings = {}
        for q, n in (("sp", 8), ("pool", 6), ("act", 4)):
            idx = [self._alloc(f"d{q}{i}") for i in range(n)]
            self.rings[q] = {"idx": idx, "tot": [0] * n, "i": 0}
        self.nps = 0
        self.out_tokens = []
        self.last_tok = {}
        self.barrier_tok = None

    def _alloc(self, name):
        self.sems.append(self.es.enter_context(self.nc.semaphore(name)))
        return len(self.sems) - 1

    def _newsem(self, e):
        i = self._alloc(f"s{e}{len(self.sems)}")
        self.own_sems.setdefault(e, set()).add(i)
        self.cur[e] = i
        self.cnt[e] = 0
        if e == "pe":
            self.pe_sems.add(i)

    def _deps(self, reads, writes, pwrites, nobarrier=False):
        deps = []
        if self.barrier_tok and not nobarrier:
            deps.append(self.barrier_tok)
        for k in reads:
            t = self.lastw.get(k)
            if t:
                deps.append(t)
            deps.extend(self.pws.get(k, {}).items())
        for k in list(writes) + list(pwrites):
            t = self.lastw.get(k)
            if t:
                deps.append(t)
            deps.extend(self.readers.get(k, {}).items())
        for k in writes:
            deps.extend(self.pws.get(k, {}).items())
        return deps

    def _commit(self, tok, reads, writes, pwrites):
        s, v = tok
        for k in reads:
            d = self.readers.setdefault(k, {})
            d[s] = max(d.get(s, 0), v)
        for k in writes:
            self.lastw[k] = tok
            self.readers[k] = {}
            self.pws[k] = {}
        for k in pwrites:
            d = self.pws.setdefault(k, {})
            d[s] = max(d.get(s, 0), v)

    def _waits(self, eng, deps):
        need = {}
        kn = self.known[eng]
        for s, v in deps:
            if eng == "pe" and s in self.pe_sems:
                continue
            if kn.get(s, 0) >= v:
                continue
            if need.get(s, 0) < v:
                need[s] = v
        for s, v in need.items():
            kn[s] = v
        return list(need.items())

    def op(self, eng, fns, reads=(), writes=(), pwrites=()):
        if self.dry:
            return
        if callable(fns):
            fns = [fns]
        lk = [k + "_lk" for k in reads if k[:2] in ("ps", "pb")]
        deps = self._deps(reads, writes, pwrites)
        if lk:
            own = self.own_sems[eng]
            deps = deps + [d for d in self._deps((), lk, ()) if d[0] not in own]
            writes = list(writes) + lk
        waits = self._waits(eng, deps)
        if self.cnt[eng] >= SEM_ROT:
            self._newsem(eng)
        self.cnt[eng] += 1
        tok = (self.cur[eng], self.cnt[eng])
        self.last_tok[eng] = tok
        self._commit(tok, reads, writes, pwrites)
        self.streams[eng].append((waits, fns, (tok[0], 1)))

    def dma(self, q, out, in_, reads=(), writes=(), pwrites=(), is_output=False, nobarrier=False):
        if self.dry:
            return
        ring = self.rings[q]
        i = ring["i"]
        ring["i"] = (i + 1) % len(ring["idx"])
        deps = self._deps(reads, writes, pwrites, nobarrier)
        if ring["tot"][i]:
            deps.append((ring["idx"][i], ring["tot"][i]))
        waits = self._waits(q, deps)
        ring["tot"][i] += 16
        tok = (ring["idx"][i], ring["tot"][i])
        self._commit(tok, reads, writes, pwrites)
        self.streams[q].append((waits, [lambda e, o=out, a=in_: e.dma_start(out=o, in_=a)], (tok[0], 16)))
        if is_output:
            self.out_tokens.append(tok)

    def fence(self, keys=None):
        if self.dry:
            return
        deps = [t for t in self.last_tok.values()]
        for ring in self.rings.values():
            for i, tot in zip(ring["idx"], ring["tot"]):
                if tot:
                    deps.append((i, tot))
        ring = self.rings["sp"]
        i = ring["i"]
        ring["i"] = (i + 1) % len(ring["idx"])
        if ring["tot"][i]:
            deps.append((ring["idx"][i], ring["tot"][i]))
        waits = self._waits("sp", deps)
        ring["tot"][i] += 16
        tok = (ring["idx"][i], ring["tot"][i])
        d0, d1 = self.fence_buf
        self.streams["sp"].append((waits, [lambda e: e.dma_start(out=d0, in_=d1)], (tok[0], 16)))
        self.barrier_tok = tok

    def finish(self):
        if self.dry:
            return
        waits = self._waits("sp", self.out_tokens)
        self.streams["sp"].append((waits, None, None))

    def emit(self):
        block = self.es.enter_context(self.nc.Block())
        sems = self.sems

        def mk(stream):
            def body(e):
                for waits, fns, inc in stream:
                    for s, v in waits:
                        e.wait_ge(sems[s], v)
                    if fns is None:
                        continue
                    ins = None
                    for f in fns:
                        ins = f(e)
                    ins.then_inc(sems[inc[0]], inc[1])
            return body

        block.tensor(mk(self.streams["pe"]))
        block.vector(mk(self.streams["dve"]))
        block.scalar(mk(self.streams["act"]))
        block.gpsimd(mk(self.streams["pool"]))
        block.sync(mk(self.streams["sp"]))


def MM(out, lhsT, rhs, start=True, stop=True):
    return lambda e: e.matmul(out, lhsT=lhsT, rhs=rhs, start=start, stop=stop)


def TR(out, in_, ident):
    return lambda e: e.transpose(out, in_, ident)


def ACT(out, in_, func, bias=None, scale=None, accum=None):
    kw = {}
    if bias is not None:
        kw["bias"] = bias
    if scale is not None:
        kw["scale"] = scale
    if accum is not None:
        kw["accum_out"] = accum
    return lambda e: e.activation(out=out, in_=in_, func=func, **kw)


def TT(out, a, b, op):
    return lambda e: e.tensor_tensor(out=out, in0=a, in1=b, op=op)


def TS(out, a, s1, op0, s2=None, op1=None):
    if op1 is None:
        return lambda e: e.tensor_scalar(out=out, in0=a, scalar1=s1, scalar2=None, op0=op0)
    return lambda e: e.tensor_scalar(out=out, in0=a, scalar1=s1, scalar2=s2, op0=op0, op1=op1)


def STT(out, a, s, b, op0, op1):
    return lambda e: e.scalar_tensor_tensor(out=out, in0=a, scalar=s, in1=b, op0=op0, op1=op1)


def CP(out, in_):
    return lambda e: e.tensor_copy(out=out, in_=in_)


def RCP(out, in_):
    return lambda e: e.reciprocal(out=out, in_=in_)


def build_program(D, DFF, NB, debug=False, stop=False):
    KC = D // P
    RW = D // 2
    GW = D // 2
    RH = RW // 256
    GH = GW // 128
    TC = NB * TB
    IN_COLS = 4 * RW + 4 * GW + 2 * GH
    GB = 4 * RW
    NWT = 3
    WEL = KC * 256
    assert (DFF // 256) * 0 == 0

    nc = bass.Bass("TRN2", target_bir_lowering=False)
    din = lambda n, s: nc.dram_tensor(n, s, F32, kind="ExternalInput").ap()
    xp_d = din("xp", [TC, D])
    xm_d = din("xm", [TC, D])
    win_d = din("w_in", [D, IN_COLS])
    wout_d = din("w_out", [D, D])
    wup_d = din("w_up", [D, DFF])
    wdn_d = din("w_down", [DFF, D])
    rope_d = din("rope", [2, P, 2 * TC])
    retc_d = din("retc", [RH, P, 256])
    zeta_d = din("zeta", [P, RH])
    gdnc_d = din("gdnc", [P, 768])
    g1t_d = din("g1t", [P, KC])
    g2t_d = din("g2t", [P, KC])
    cw_d = din("cw", [P, 4 * 3 * GH])
    rgain_d = din("rgain", [RW])
    ggain_d = din("ggain", [128])
    alog_d = din("alog", [GH])
    dtb_d = din("dtb", [GH])
    fgain_d = din("fgain", [D])
    out_d = nc.dram_tensor("out", [TC, D], F32, kind="ExternalOutput").ap()
    dbg_d = nc.dram_tensor("dbg", [D, TC], BF16, kind="ExternalOutput").ap() if debug else None

    es = ExitStack()
    S = Sched(nc, es)
    sb = lambda n, s, d=F32: es.enter_context(nc.sbuf_tensor("sb_" + n, s, d))

    hT_t = sb("hT", [P, KC * TB], BF16)
    hT = hT_t[:].rearrange("p (k t) -> p k t", k=KC)
    FG_OFF = max(D, 2048)
    YR = sb("YR", [P, max(KC * TB // 2, FG_OFF + D)], F32)
    yT = YR[:, 0:KC * TB // 2].bitcast(BF16).rearrange("p (k t) -> p k t", k=KC)
    xsb = [YR[:, i * (D // 2):(i + 1) * (D // 2)].bitcast(BF16) for i in range(2)]
    fgain = YR[:, FG_OFF:FG_OFF + D]
    aT = YR[:, 0:4 * TB // 2].bitcast(BF16).rearrange("p (k t) -> p k t", k=4)
    ARN = max(NT * D, 16384)
    r_relu = [YR[:, 1024 + ii * TB:1024 + (ii + 1) * TB] for ii in range(2)]
    AR = sb("AR", [P, ARN], F32)
    acc = AR[:, 0:NT * D].rearrange("p (t d) -> p t d", t=NT)
    wt_t = [sb(f"wt{i}", [P, WEL], BF16) for i in range(NWT)]
    Rst = sb("Rst", [P, RH * 512], F32)
    Sst = sb("Sst", [P, GH * 128], F32)
    halo = sb("halo", [P, 3 * GH * 3], F32)
    g1t = sb("g1t", [P, KC])
    g2t = sb("g2t", [P, KC])
    cw = sb("cw", [P, 12 * GH])
    zeta = sb("zeta", [P, RH])
    identb = sb("identb", [P, P], BF16)
    small = sb("small", [P, 64])
    S.fence_buf = (small[:, 60:61], small[:, 61:62])
    ss = small[:, 0:8]
    rstd = small[:, 8:16]
    gst = small[:, 16:32]
    negA = sb("negA", [P, GH])
    dtb = sb("dtb", [P, GH])

    class Carver:
        def __init__(self, base, limit):
            self.o, self.limit = base, limit

        def f32(self, n):
            a = AR[:, self.o:self.o + n]
            self.o += n
            assert self.o <= self.limit, (self.o, self.limit)
            return a

        def bf16(self, n):
            assert n % 2 == 0
            a = AR[:, self.o:self.o + n // 2].bitcast(BF16)
            self.o += n // 2
            assert self.o <= self.limit, (self.o, self.limit)
            return a

    cv = Carver(0, ARN)
    gdnc = cv.f32(768)
    Lc, Uc, identf, ones = (gdnc[:, i * 128:(i + 1) * 128] for i in range(4))
    NEG2 = gdnc[:, 512:768]
    ropec = cv.f32(2 * TB).rearrange("p (c t) -> p c t", c=2)
    ggain = cv.f32(128)
    GDN_BASE = cv.o
    rgain = cv.f32(RW)
    CONST_END = cv.o
    xt = [AR[:, CONST_END + i * D:CONST_END + (i + 1) * D] for i in range(2)]
    assert CONST_END + 2 * D <= ARN
    cr = Carver(CONST_END, ARN)
    r_tmp = [cr.f32(TB) for _ in range(4)]
    r_retc = [cr.f32(256) for _ in range(2)]
    r_qr = [cr.bf16(2 * TB).rearrange("p (c t) -> p c t", c=2) for _ in range(2)]
    r_qx = [cr.bf16(2 * TB).rearrange("p (c t) -> p c t", c=2) for _ in range(2)]
    r_kr = [cr.bf16(2 * TB).rearrange("p (c t) -> p c t", c=2) for _ in range(2)]
    r_v = [[cr.bf16(256) for _ in range(NT)] for _ in range(2)]
    r_g1 = [cr.f32(256) for _ in range(NT)]
    r_g = [r_g1, r_g1]
    r_kd = [[cr.bf16(256) for _ in range(NT)] for _ in range(2)]
    r_stm = [cr.bf16(128) for _ in range(2)]
    r_yn = [cr.f32(256) for _ in range(2)]
    r_yb = [cr.bf16(256) for _ in range(2)]
    r_sq = cr.f32(256)
    r_rbf = cr.bf16(512)
    r_st = cr.f32(16)
    RET_KEYS = []
    cg = Carver(GDN_BASE, ARN)
    g_u2 = [cg.f32(TB + 4) for _ in range(2)]
    g_acc2 = [cg.f32(TB) for _ in range(2)]
    g_cs2 = [cg.f32(TB) for _ in range(2)]
    g_u, g_acc, g_cs, g_rs = g_u2[0], g_acc2[0], g_cs2[0], g_u2[1]
    g_q = [cg.f32(TB) for _ in range(2)]
    g_k = [cg.f32(TB) for _ in range(2)]
    g_v = [cg.f32(TB) for _ in range(2)]
    g_gz = [cg.f32(256) for _ in range(NT)]
    g_beta = [cg.f32(GH) for _ in range(NT)]
    g_g = [cg.f32(GH) for _ in range(NT)]
    g_eg = [cg.f32(3 * GH) for _ in range(NT)]
    g_bw = [cg.f32(GH) for _ in range(NT)]
    g_xa = cg.f32(GH)
    g_e1 = cg.f32(GH)
    NBUF = 4
    g_rhsu = [cg.f32(128) for _ in range(NBUF)]
    g_rhsw = [cg.f32(128) for _ in range(NBUF)]
    g_kdec = [cg.f32(128) for _ in range(NBUF)]
    g_G2 = [cg.f32(128) for _ in range(NBUF)]
    g_dec = [cg.f32(256) for _ in range(NBUF)]
    g_dm = g_dec
    g_L = [cg.f32(128) for _ in range(NBUF)]
    g_UP = [cg.f32(256) for _ in range(NBUF)]
    g_attn = [cg.f32(128) for _ in range(NBUF)]
    g_uw = [cg.f32(256) for _ in range(NBUF)]
    g_vnew, g_a1, g_o = g_G2, g_rhsu, g_rhsw
    g_junk2 = [g_dec[ii][:, 0:128] for ii in range(NBUF)]
    g_yb = [cg.bf16(128) for _ in range(NBUF)]
    g_st = cg.f32(2 * NBUF)

    NPS = 6
    psf = [es.enter_context(nc.psum_tensor(f"psf{i}", [P, 512], F32)) for i in range(NPS)]
    psb = [es.enter_context(nc.psum_tensor(f"psb{i}", [P, 1024], BF16)) for i in range(2)]
    st = {"ps": 0, "pb": 0, "wi": 0, "wrel": 0}

    held = {}

    def PS(hold=False):
        for _ in range(NPS):
            i = st["ps"]
            st["ps"] = (i + 1) % NPS
            if not held.get(f"ps{i}"):
                if hold:
                    held[f"ps{i}"] = True
                return psf[i], f"ps{i}"
        assert not hold, "no free PSUM bank"
        return None, None

    def PB(hold=False):
        for _ in range(2):
            i = st["pb"]
            st["pb"] = 1 - i
            if not held.get(f"pb{i}"):
                if hold:
                    held[f"pb{i}"] = True
                return psb[i], f"pb{i}"
        return None, None

    def PREL(k):
        held[k] = False

    wplan = []

    def WGET(src, kc, cols):
        n = st["wi"]
        st["wi"] = n + 1
        buf = wt_t[n % NWT]
        view = buf[:, 0:kc * cols].rearrange("p (k c) -> p k c", k=kc)
        if S.dry:
            wplan.append((src, kc, cols))
            return view, f"w{n % NWT}"
        assert wplan[n][1:] == (kc, cols)
        assert n < st["wrel"] + NWT, "too many live weight tiles"
        if n == 0:
            for j in range(min(NWT, len(wplan))):
                _wload(j)
        return view, f"w{n % NWT}"

    def WREL():
        m = st["wrel"]
        st["wrel"] = m + 1
        if S.dry:
            return
        if m + NWT < len(wplan):
            _wload(m + NWT)

    def _wload(j):
        src, kc, cols = wplan[j]
        buf = wt_t[j % NWT]
        dst = buf[:, 0:kc * cols].rearrange("p (k c) -> p k c", k=kc)
        S.dma("pool", dst, src.rearrange("(k p) c -> p k c", p=P), writes=[f"w{j % NWT}"], nobarrier=True)

    def rstd_from_ss(ss_ap, n, dst):
        S.op("act", ACT(dst, ss_ap, AF.Sqrt, bias=EPS, scale=1.0 / n), reads=["ssb"], writes=["rsb"])
        S.op("dve", RCP(dst, dst), reads=["rsb"], writes=["rsb"])

    def hT_tile(t, xa, skey, gT, dst, dst_key):
        xb = xsb[t % 2]
        xk = f"xsb{t % 2}"
        S.op("act", ACT(xb, xa, AF.Square, accum=ss[:, t:t + 1]), reads=[skey], writes=[xk, f"ssb{t}"])
        yield
        S.op("act", ACT(rstd[:, t:t + 1], ss[:, t:t + 1], AF.Sqrt, bias=EPS, scale=1.0 / D), reads=[f"ssb{t}"], writes=[f"rsb{t}"])
        yield
        S.op("dve", RCP(rstd[:, t:t + 1], rstd[:, t:t + 1]), reads=[f"rsb{t}"], writes=[f"rsb{t}"])
        yield
        S.op("dve", TS(xb, xa, rstd[:, t:t + 1], ALU.mult), reads=[skey, f"rsb{t}"], writes=[xk])
        yield
        for k0 in range(0, KC, 4):
            pb, pk = PB(True)
            while pb is None:
                yield
                pb, pk = PB(True)
            S.op("pe", [TR(pb[:, j * 128:(j + 1) * 128], xb[:, (k0 + j) * 128:(k0 + j + 1) * 128], identb[:])
                        for j in range(4)], reads=[xk], writes=[pk])
            yield
            if (k0 // 4) % 2 == 1:
                for j in range(4):
                    S.op("act", ACT(dst[:, k0 + j, t * 128:(t + 1) * 128], pb[:, j * 128:(j + 1) * 128], AF.Copy,
                                    scale=gT[:, k0 + j:k0 + j + 1]), reads=[pk], pwrites=[dst_key])
            else:
                S.op("dve", TT(dst[:, k0:k0 + 4, t * 128:(t + 1) * 128],
                               pb[:, 0:512].rearrange("p (k t) -> p k t", k=4),
                               gT[:, k0:k0 + 4].unsqueeze(2).to_broadcast([P, 4, 128]), ALU.mult),
                     reads=[pk], pwrites=[dst_key])
            PREL(pk)
            yield

    def make_hT(src_tile_fn, src_keys, gT, dst, dst_key, load=None):
        for t0 in range(0, NT, 2):
            if load is not None:
                load(t0)
                load(t0 + 1)
            lockstep([hT_tile(t, src_tile_fn(t), src_keys[t], gT, dst, dst_key) for t in (t0, t0 + 1)])

    def proj_fm(w, wk, c0, src, src_key, hold=False):
        ps, pk = PS(hold)
        S.op("pe", [MM(ps[:, 0:TB], w[:, kc, c0:c0 + 128], src[:, kc, :], kc == 0, kc == KC - 1) for kc in range(KC)],
             reads=[wk, src_key], writes=[pk])
        return ps, pk

    def proj_tm(w, wk, t, ncols, src, src_key, c0=0):
        ps, pk = PS()
        S.op("pe", [MM(ps[:, 0:ncols], src[:, kc, t * 128:(t + 1) * 128], w[:, kc, c0:c0 + ncols], kc == 0, kc == KC - 1)
                    for kc in range(KC)], reads=[wk, src_key], writes=[pk])
        return ps, pk

    def ret_head(main, h, tok_off):
        b = h % 2
        kq, kx, kk = f"rqr{b}", f"rqx{b}", f"rkr{b}"
        retc = r_retc[b]
        S.dma("sp", retc, retc_d[h], writes=[f"retc{b}"])
        DT = retc[:, 0:128]
        XI = retc[:, 128:256]
        cos = ropec[:, 0, :]
        sin = ropec[:, 1, :]

        def rope(wcol0, dst, dkey, with_xi):
            w, wk = WGET(win_d[:, wcol0:wcol0 + 256], KC, 256)
            p1, k1 = proj_fm(w, wk, 0, hT, "hT")
            p2, k2 = proj_fm(w, wk, 128, hT, "hT")
            WREL()
            t1, t2, t3, t4 = r_tmp
            S.op("dve", TT(t1, p1[:, 0:TB], cos, ALU.mult), reads=[k1, "ropec"], writes=["rt1"])
            S.op("dve", TT(t2, p2[:, 0:TB], sin, ALU.mult), reads=[k2, "ropec"], writes=["rt2"])
            S.op("dve", TT(t3, p1[:, 0:TB], sin, ALU.mult), reads=[k1, "ropec"], writes=["rt3"])
            S.op("dve", TT(t4, p2[:, 0:TB], cos, ALU.mult), reads=[k2, "ropec"], writes=["rt4"])
            if not with_xi:
                S.op("pool", TT(dst[:, 0, :], t1, t2, ALU.subtract), reads=["rt1", "rt2"], pwrites=[dkey])
                S.op("pool", TT(dst[:, 1, :], t3, t4, ALU.add), reads=["rt3", "rt4"], pwrites=[dkey])
                return
            xi_b = XI.unsqueeze(1).to_broadcast([P, NT, 128])
            v3 = lambda a: a.rearrange("p (t c) -> p t c", t=NT)
            S.op("pool", TT(t1, t1, t2, ALU.subtract), reads=["rt1", "rt2"], writes=["rt1"])
            S.op("pool", TT(t3, t3, t4, ALU.add), reads=["rt3", "rt4"], writes=["rt3"])
            S.op("act", ACT(dst[:, 0, :], t1, AF.Copy), reads=["rt1"], pwrites=[dkey])
            S.op("act", ACT(dst[:, 1, :], t3, AF.Copy), reads=["rt3"], pwrites=[dkey])
            S.op("pool", TT(v3(r_qx[b][:, 0, :]), v3(t1), xi_b, ALU.mult), reads=["rt1", f"retc{b}"], pwrites=[kx])
            S.op("pool", TT(v3(r_qx[b][:, 1, :]), v3(t3), xi_b, ALU.mult), reads=["rt3", f"retc{b}"], pwrites=[kx])

        if main:
            rope(h * 256, r_qr[b], kq, True)
        rope(RW + h * 256, r_kr[b], kk, False)
        wv, wvk = WGET(win_d[:, 2 * RW + h * 256:2 * RW + (h + 1) * 256], KC, 256)
        for t in range(NT):
            ps, pk = proj_tm(wv, wvk, t, 256, hT, "hT")
            S.op("act", ACT(r_v[b][t], ps[:, 0:256], AF.Copy), reads=[pk], writes=[f"rv{b}{t}"])
        WREL()
        if main:
            wg, wgk = WGET(win_d[:, 3 * RW + h * 256:3 * RW + (h + 1) * 256], KC, 256)
            for t in range(NT):
                ps, pk = proj_tm(wg, wgk, t, 256, hT, "hT")
                S.op("act", ACT(r_g[b][t], ps[:, 0:256], AF.Silu), reads=[pk], writes=[f"rg{t}"])
                S.op("pool", TT(r_g[b][t], r_g[b][t], rgain[:, h * 256:(h + 1) * 256], ALU.mult),
                     reads=[f"rg{t}", "rgain"], writes=[f"rg{t}"])
            WREL()
        for t in range(NT):
            pb, pk = PB()
            S.op("pe", [TR(pb[:, dc * 128:(dc + 1) * 128], r_kr[b][:, dc, t * 128:(t + 1) * 128], identb[:]) for dc in range(2)],
                 reads=[kk], writes=[pk])
            S.op("act", ACT(r_kd[b][t], pb[:, 0:256], AF.Copy, scale=zeta[:, h:h + 1]), reads=[pk], writes=[f"rkd{b}{t}"])
        R = Rst[:, h * 512:(h + 1) * 512]
        Rk = f"R{h}"
        g128 = float(np.exp(np.log1p(-2.0 ** (-5.0 - h)) * 128.0))
        S.op("act", ACT(r_rbf, R, AF.Copy), reads=[Rk], writes=["rbf"])
        for t in range(NT):
            tsl = slice(t * 128, (t + 1) * 128)
            if main:
                ps_s, ks = PS()
                S.op("pe", [MM(ps_s[:, 0:128], r_kr[b][:, dc, tsl], r_qr[b][:, dc, tsl], dc == 0, dc == 1) for dc in range(2)],
                     reads=[kk, kq], writes=[ks])
                stm = r_stm[t % 2]
                S.op("dve", TT(stm, ps_s[:, 0:128], DT, ALU.mult), reads=[ks, f"retc{b}"], writes=[f"stm{t % 2}"])
                ps_o, ko = PS()
                S.op("pe", [MM(ps_o[:, 0:256], stm, r_v[b][t], True, False),
                            MM(ps_o[:, 0:256], r_qx[b][:, 0, tsl], r_rbf[:, 0:256], False, False),
                            MM(ps_o[:, 0:256], r_qx[b][:, 1, tsl], r_rbf[:, 256:512], False, True)],
                     reads=[f"stm{t % 2}", f"rv{b}{t}", kx, "rbf"], writes=[ko])
            ps_r, kr_ = PS()
            S.op("pe", [MM(ps_r[:, dc * 256:(dc + 1) * 256], r_kd[b][t][:, dc * 128:(dc + 1) * 128], r_v[b][t], True, True)
                        for dc in range(2)], reads=[f"rkd{b}{t}", f"rv{b}{t}"], writes=[kr_])
            S.op("dve", STT(R, R, g128, ps_r[:, 0:512], ALU.mult, ALU.add), reads=[kr_, Rk], writes=[Rk])
            S.op("act", ACT(r_rbf, R, AF.Copy), reads=[Rk], writes=["rbf"])
            if main:
                yn = r_yn[t % 2]
                ynk = f"ryn{t % 2}"
                S.op("act", ACT(yn, ps_o[:, 0:256], AF.Identity, accum=r_st[:, 0:1]), reads=[ko], writes=[ynk, "rst0"])
                S.op("act", ACT(r_sq, ps_o[:, 0:256], AF.Square, accum=r_st[:, 1:2]), reads=[ko], writes=["rsq", "rst1"])
                S.op("dve", TS(r_st[:, 2:3], r_st[:, 0:1], 1.0 / 256, ALU.mult), reads=["rst0"], writes=["rst2"])
                S.op("dve", TT(r_st[:, 3:4], r_st[:, 2:3], r_st[:, 2:3], ALU.mult), reads=["rst2"], writes=["rst3"])
                S.op("dve", STT(r_st[:, 4:5], r_st[:, 1:2], 1.0 / 256, r_st[:, 3:4], ALU.mult, ALU.subtract),
                     reads=["rst1", "rst3"], writes=["rst4"])
                S.op("act", ACT(r_st[:, 4:5], r_st[:, 4:5], AF.Sqrt, bias=EPS), reads=["rst4"], writes=["rst4"])
                S.op("dve", RCP(r_st[:, 4:5], r_st[:, 4:5]), reads=["rst4"], writes=["rst4"])
                S.op("dve", TS(yn, yn, r_st[:, 2:3], ALU.subtract, r_st[:, 4:5], ALU.mult), reads=[ynk, "rst2", "rst4"], writes=[ynk])
                yb = r_yb[t % 2]
                S.op("pool", TT(yb, yn, r_g[b][t], ALU.mult), reads=[ynk, f"rg{t}"], writes=[f"ryb{t % 2}"])
                pb, pk = PB()
                S.op("pe", [TR(pb[:, dc * 128:(dc + 1) * 128], yb[:, dc * 128:(dc + 1) * 128], identb[:]) for dc in range(2)],
                     reads=[f"ryb{t % 2}"], writes=[pk])
                S.op("act", ACT(yT[:, 2 * h:2 * h + 2, tsl], pb[:, 0:256].rearrange("p (k t) -> p k t", k=2), AF.Copy),
                     reads=[pk], pwrites=["yT"])

    def gdn_gates(tok_off):
        w, wk = WGET(win_d[:, GB + 4 * GW:GB + 4 * GW + 2 * GH], KC, 2 * GH)
        for t in range(NT):
            ps, pk = proj_tm(w, wk, t, 2 * GH, hT, "hT")
            S.op("act", ACT(g_beta[t], ps[:, 0:GH], AF.Sigmoid), reads=[pk], writes=[f"gbeta{t}"])
            S.op("dve", TT(g_xa, ps[:, GH:2 * GH], dtb[:], ALU.add), reads=[pk], writes=["gxa"])
            S.op("act", ACT(g_e1, g_xa, AF.Abs), reads=["gxa"], writes=["ge1"])
            S.op("act", ACT(g_e1, g_e1, AF.Exp, scale=-1.0), reads=["ge1"], writes=["ge1"])
            S.op("act", ACT(g_e1, g_e1, AF.Ln, bias=1.0), reads=["ge1"], writes=["ge1"])
            S.op("dve", STT(g_xa, g_xa, 0.0, g_e1, ALU.max, ALU.add), reads=["gxa", "ge1"], writes=["gxa"])
            S.op("dve", TT(g_g[t], g_xa, negA[:], ALU.mult), reads=["gxa"], writes=[f"gg{t}"])
            ps2, pk2 = PS()
            S.op("pe", [MM(ps2[:, 0:GH], Lc, g_g[t]), MM(ps2[:, GH:2 * GH], Uc, g_g[t]), MM(ps2[:, 2 * GH:3 * GH], ones, g_g[t])],
                 reads=[f"gg{t}", "gdnc"], writes=[pk2])
            S.op("act", ACT(g_eg[t], ps2[:, 0:3 * GH], AF.Exp), reads=[pk2], writes=[f"geg{t}"])
            S.op("dve", TT(g_bw[t], g_beta[t], g_eg[t][:, 0:GH], ALU.mult), reads=[f"gbeta{t}", f"geg{t}"], writes=[f"gbw{t}"])
        WREL()

    def gdn_feat(w, wk, j, chunk, dst, dkey, mode, fi):
        gu, gacc, gcs = g_u2[fi], g_acc2[fi], g_cs2[fi]
        ku, ka, kc_ = f"gu{fi}", f"gacc{fi}", f"gcs{fi}"
        ps, pk = proj_fm(w, wk, j * 128, hT, "hT", hold=True)
        hl = halo[:, chunk * 3:(chunk + 1) * 3]
        hk = f"halo{chunk}"
        yield
        if mode == "halo":
            S.op("act", ACT(hl, ps[:, TB - 3:TB], AF.Copy), reads=[pk], writes=[hk])
            PREL(pk)
            return
        S.op("act", ACT(gu[:, 3:3 + TB], ps[:, 0:TB], AF.Copy), reads=[pk], writes=[ku])
        PREL(pk)
        S.op("act", ACT(gu[:, 0:3], hl, AF.Copy), reads=[hk], pwrites=[ku])
        yield
        S.op("act", ACT(hl, gu[:, TB:TB + 3], AF.Copy), reads=[ku], writes=[hk])
        cwc = lambda jj: cw[:, jj * 3 * GH + chunk:jj * 3 * GH + chunk + 1]
        S.op("act", ACT(gacc, gu[:, 3:3 + TB], AF.Copy, scale=cwc(3)), reads=[ku], writes=[ka])
        yield
        for jj in (2, 1, 0):
            S.op("dve", STT(gacc, gu[:, jj:jj + TB], cwc(jj), gacc, ALU.mult, ALU.add), reads=[ku, ka], writes=[ka])
            yield
        if mode == "v":
            S.op("act", ACT(dst, gacc, AF.Silu), reads=[ka], writes=[dkey])
            return
        S.op("act", ACT(gcs, gacc, AF.Silu), reads=[ka], writes=[kc_])
        yield
        gsq = gacc
        S.op("pool", TT(gsq, gcs, gcs, ALU.mult), reads=[kc_], writes=[ka])
        yield
        ps2, pk2 = PS(True)
        S.op("pe", [MM(ps2[:, hh * 256:(hh + 1) * 256], ones, gsq[:, hh * 256:(hh + 1) * 256]) for hh in range(TB // 256)],
             reads=[ka, "gdnc"], writes=[pk2])
        yield
        grs = gu[:, 0:TB]
        S.op("act", ACT(grs, ps2[:, 0:TB], AF.Sqrt, bias=EPS), reads=[pk2], writes=[ku])
        PREL(pk2)
        yield
        S.op("dve", RCP(grs, grs), reads=[ku], writes=[ku])
        yield
        if mode == "q":
            S.op("dve", STT(dst, gcs, float(128 ** -0.5), grs, ALU.mult, ALU.mult), reads=[kc_, ku], writes=[dkey])
        else:
            S.op("pool", TT(dst, gcs, grs, ALU.mult), reads=[kc_, ku], writes=[dkey])

    cnt = {"ch": 0}

    def gdn_chunk(main, hd, t, b, i):
        tsl = slice(t * 128, (t + 1) * 128)
        hs = slice(hd, hd + 1)
        kT, qT, vT = g_k[b], g_q[b], g_v[b]
        K = lambda n: f"{n}{i}"
        ps, pk = PS(True)
        S.op("pe", [TR(ps[:, 0:128], kT[:, tsl], identf), TR(ps[:, 128:256], vT[:, tsl], identf)],
             reads=[f"gk{b}", f"gv{b}", "gdnc"], writes=[pk])
        S.op("act", ACT(g_G2[i], Uc, AF.Copy, scale=g_g[t][:, hs]), reads=["gdnc", f"gg{t}"], writes=[K("G2")])
        yield
        S.op("act", ACT(g_rhsw[i], ps[:, 0:128], AF.Copy, scale=g_bw[t][:, hs]), reads=[pk, f"gbw{t}"], writes=[K("rhsw")])
        S.op("act", ACT(g_kdec[i], ps[:, 0:128], AF.Copy, scale=g_eg[t][:, GH + hd:GH + hd + 1]), reads=[pk, f"geg{t}"], writes=[K("kdec")])
        S.op("act", ACT(g_rhsu[i], ps[:, 128:256], AF.Copy, scale=g_beta[t][:, hs]), reads=[pk, f"gbeta{t}"], writes=[K("rhsu")])
        PREL(pk)
        psD, kD = PS(True)
        mms = [MM(psD[:, 0:128], Lc, g_G2[i]), MM(psD[:, 128:256], g_G2[i], Lc), MM(psD[:, 256:384], kT[:, tsl], kT[:, tsl])]
        if main:
            mms.append(MM(psD[:, 384:512], kT[:, tsl], qT[:, tsl]))
        S.op("pe", mms, reads=[K("G2"), "gdnc", f"gk{b}"] + ([f"gq{b}"] if main else []), writes=[kD])
        yield
        S.op("dve", STT(g_dm[i], psD[:, 0:256], 0.0, NEG2, ALU.min, ALU.add), reads=[kD, "gdnc"], writes=[K("dec")])
        yield
        S.op("act", ACT(g_dec[i], g_dm[i], AF.Exp), reads=[K("dec")], writes=[K("dec")])
        yield
        L, UP = g_L[i], g_UP[i]
        S.op("dve", STT(L, psD[:, 256:384], g_beta[t][:, hs], g_dec[i][:, 0:128], ALU.mult, ALU.mult),
             reads=[kD, K("dec"), f"gbeta{t}"], writes=[K("L")])
        if main:
            S.op("dve", TT(g_attn[i], psD[:, 384:512], g_dec[i][:, 128:256], ALU.mult), reads=[kD, K("dec")], writes=[K("attn")])
        PREL(kD)
        yield
        psU, kU = PS(True)
        S.op("pe", TR(psU[:, 0:128], L, identf), reads=[K("L"), "gdnc"], writes=[kU])
        yield
        S.op("act", ACT(UP[:, 0:128], psU[:, 0:128], AF.Copy), reads=[kU], pwrites=[K("UP")])
        S.op("dve", STT(UP[:, 128:256], psU[:, 0:128], -1.0, identf, ALU.mult, ALU.add), reads=[kU, "gdnc"], pwrites=[K("UP")])
        PREL(kU)
        yield
        Pm = UP[:, 128:256]
        for s_ in range(6):
            psA, kA = PS(True)
            S.op("pe", [MM(psA[:, 0:256], L, UP), MM(psA[:, 256:384], UP[:, 0:128], L)], reads=[K("L"), K("UP")], writes=[kA])
            yield
            S.op("act", ACT(UP[:, 0:128], psA[:, 0:128], AF.Copy), reads=[kA], pwrites=[K("UP")])
            S.op("act", ACT(L, psA[:, 256:384], AF.Copy), reads=[kA], writes=[K("L")])
            if s_ > 0:
                S.op("dve", TT(Pm, psA[:, 128:256], Pm, ALU.add), reads=[kA], pwrites=[K("UP")])
            PREL(kA)
            yield
        psA, kA = PS(True)
        S.op("pe", MM(psA[:, 0:128], L, Pm), reads=[K("L"), K("UP")], writes=[kA])
        yield
        S.op("dve", TT(Pm, psA[:, 0:128], Pm, ALU.add), reads=[kA, K("UP")], writes=[K("UP")])
        PREL(kA)
        yield
        psu, ku = PS(True)
        S.op("pe", [MM(psu[:, 0:128], Pm, g_rhsu[i]), MM(psu[:, 128:256], g_rhsw[i], Pm)],
             reads=[K("UP"), K("rhsu"), K("rhsw")], writes=[ku])
        yield
        S.op("act", ACT(g_uw[i], psu[:, 0:256], AF.Copy), reads=[ku], writes=[K("uw")])
        PREL(ku)
        yield
        while t > 0 and not rec_done.get((hd, t - 1)):
            yield
        Sm = Sst[:, hd * 128:(hd + 1) * 128]
        Sk = f"S{hd}"
        psS, kS = PS(True)
        mms = [MM(psS[:, 0:128], g_uw[i][:, 128:256], Sm)]
        if main:
            mms.append(MM(psS[:, 128:256], qT[:, tsl], Sm))
        S.op("pe", mms, reads=[K("uw"), Sk] + ([f"gq{b}"] if main else []), writes=[kS])
        yield
        S.op("dve", STT(g_vnew[i], psS[:, 0:128], -1.0, g_uw[i][:, 0:128], ALU.mult, ALU.add), reads=[K("uw"), kS], writes=[K("G2")])
        if main:
            S.op("act", ACT(g_a1[i], psS[:, 128:256], AF.Copy, scale=g_eg[t][:, hs]), reads=[kS, f"geg{t}"], writes=[K("rhsu")])
        PREL(kS)
        yield
        psO, kO = PS(True)
        mms = [MM(psO[:, 128:256], g_kdec[i], g_vnew[i])]
        if main:
            mms.append(MM(psO[:, 0:128], g_attn[i], g_vnew[i]))
        S.op("pe", mms, reads=[K("kdec"), K("G2")] + ([K("attn")] if main else []), writes=[kO])
        yield
        S.op("dve", STT(Sm, Sm, g_eg[t][:, 2 * GH + hd:2 * GH + hd + 1], psO[:, 128:256], ALU.mult, ALU.add),
             reads=[kO, Sk, f"geg{t}"], writes=[Sk])
        rec_done[(hd, t)] = True
        if not main:
            PREL(kO)
        if main:
            S.op("dve", TT(g_o[i], psO[:, 0:128], g_a1[i], ALU.add), reads=[kO, K("rhsu")], writes=[K("rhsw")])
            PREL(kO)
            yield
            S.op("act", ACT(g_junk2[i], g_o[i], AF.Square, accum=g_st[:, 2 * i:2 * i + 1]), reads=[K("rhsw")], writes=[K("dec"), K("gst0")])
            yield
            r1 = g_st[:, 2 * i + 1:2 * i + 2]
            S.op("act", ACT(r1, g_st[:, 2 * i:2 * i + 1], AF.Sqrt, bias=EPS, scale=1.0 / 128), reads=[K("gst0")], writes=[K("gst1")])
            yield
            S.op("dve", RCP(r1, r1), reads=[K("gst1")], writes=[K("gst1")])
            jj = hd % 2
            S.op("dve", STT(g_yb[i], g_o[i], r1, g_gz[t][:, jj * 128:(jj + 1) * 128], ALU.mult, ALU.mult),
                 reads=[K("rhsw"), K("gst1"), f"ggz{t}"], writes=[K("yb")])
            yield
            pb, pk = PB(True)
            while pb is None:
                yield
                pb, pk = PB(True)
            S.op("pe", TR(pb[:, 0:128], g_yb[i], identb[:]), reads=[K("yb")], writes=[pk])
            yield
            S.op("act", ACT(yT[:, RW // 128 + hd, tsl], pb[:, 0:128], AF.Copy), reads=[pk], pwrites=["yT"])
            PREL(pk)

    rec_done = {}

    def lockstep(gens):
        gens = list(gens)
        while gens:
            for g in list(gens):
                try:
                    next(g)
                except StopIteration:
                    gens.remove(g)

    def gdn_pair(main, pair, last_prefix):
        c0 = pair * 256
        if main:
            wz, wzk = WGET(win_d[:, GB + 3 * GW + c0:GB + 3 * GW + c0 + 256], KC, 256)
            for t in range(NT):
                ps, pk = proj_tm(wz, wzk, t, 256, hT, "hT")
                S.op("act", ACT(g_gz[t], ps[:, 0:256], AF.Silu), reads=[pk], writes=[f"ggz{t}"])
                S.op("pool", TT(g_gz[t].rearrange("p (j c) -> p j c", j=2), g_gz[t].rearrange("p (j c) -> p j c", j=2),
                                ggain.unsqueeze(1).to_broadcast([P, 2, 128]), ALU.mult), reads=[f"ggz{t}", "ggain"], writes=[f"ggz{t}"])
            WREL()
        if main or last_prefix:
            wq, wqk = WGET(win_d[:, GB + c0:GB + c0 + 256], KC, 256)
            lockstep([gdn_feat(wq, wqk, j, pair * 2 + j, g_q[j] if main else None, f"gq{j}", "q" if main else "halo", j)
                      for j in range(2)])
            WREL()
        wk_, wkk = WGET(win_d[:, GB + GW + c0:GB + GW + c0 + 256], KC, 256)
        lockstep([gdn_feat(wk_, wkk, j, GH + pair * 2 + j, g_k[j], f"gk{j}", "k", j) for j in range(2)])
        WREL()
        wv, wvk = WGET(win_d[:, GB + 2 * GW + c0:GB + 2 * GW + c0 + 256], KC, 256)
        lockstep([gdn_feat(wv, wvk, j, 2 * GH + pair * 2 + j, g_v[j], f"gv{j}", "v", j) for j in range(2)])
        WREL()
        rec_done.clear()
        for t0 in range(0, NT, 2):
            lockstep([gdn_chunk(main, pair * 2 + j, t, j, 2 * (t % 2) + j) for t in (t0, t0 + 1) for j in range(2)])

    ALLW = [f"w{i}" for i in range(NWT)]

    def load_block_consts(tok_off_rope):
        S.dma("sp", gdnc, gdnc_d, writes=["gdnc"])
        S.dma("sp", ropec, rope_d[:, :, tok_off_rope:tok_off_rope + TB].rearrange("c p t -> p c t"), writes=["ropec"])
        S.dma("sp", rgain, rgain_d.partition_broadcast(P), writes=["rgain"])
        S.dma("sp", ggain, ggain_d.partition_broadcast(P), writes=["ggain"])

    SCR = ["AR"]

    def block(main, bi):
        x_d = xm_d if main else xp_d
        r0 = bi * TB
        at = lambda stage: isinstance(stop, tuple) and stop == (main, bi, stage)
        S.fence(["AR", "acc0", "acc1", "acc2", "acc3", "xt0", "xt1"])
        load_block_consts((TC if main else 0) + r0)
        def load_x(t):
            S.dma("sp", xt[t % 2], x_d[r0 + t * 128:r0 + (t + 1) * 128, :], writes=[f"xt{t % 2}"])

        make_hT(lambda t: xt[t % 2], [f"xt{t % 2}" for t in range(NT)], g1t, hT, "hT", load_x)
        if at("hT"):
            return "STOP"
        S.fence(["xt0", "xt1", "AR"])
        for h in range(RH):
            ret_head(main, h, r0)
        if at("ret"):
            return "STOP"
        S.fence(["AR", "rt1", "rt2", "rt3", "rt4"])
        gdn_gates(r0)
        if at("gates"):
            return "STOP"
        last_prefix = (not main) and bi == NB - 1
        for pair in range(GH // 2):
            if gdn_pair(main, pair, last_prefix) == "STOP":
                return "STOP"
        if at("gdn"):
            return "STOP"
        if not main:
            return
        if stop == True or at("gdn"):
            return "STOP"
        S.fence(["AR"] + [f"acc{t}" for t in range(NT)])
        for t in range(NT):
            S.dma("sp", acc[:, t, :], x_d[r0 + t * 128:r0 + (t + 1) * 128, :], writes=[f"acc{t}"])
        if debug:
            S.dma("sp", dbg_d.rearrange("(k p) t -> p k t", p=P)[:, :, r0:r0 + TB], yT, reads=["yT"], writes=["dbgout"], is_output=True)
        for cgi in range(D // 256):
            w, wk = WGET(wout_d[:, cgi * 256:(cgi + 1) * 256], KC, 256)
            for t in range(NT):
                ps, pk = proj_tm(w, wk, t, 256, yT, "yT")
                sl = acc[:, t, cgi * 256:(cgi + 1) * 256]
                S.op("dve", TT(sl, ps[:, 0:256], sl, ALU.add), reads=[pk], writes=[f"acc{t}"])
            WREL()
        if at("outproj"):
            return "STOP"
        S.fence(["yT", "xsb0", "xsb1", "hT", "dbgout"])
        make_hT(lambda t: acc[:, t, :], [f"acc{t}" for t in range(NT)], g2t, hT, "hT")
        S.fence(["xsb0", "xsb1", "aT", "hT"])
        S.dma("sp", fgain, fgain_d.partition_broadcast(P), writes=["fgain"])
        for F in range(DFF // 512):
            for half in range(2):
                wu, wuk = WGET(wup_d[:, F * 512 + half * 256:F * 512 + (half + 1) * 256], KC, 256)
                for dc in range(2):
                    ps, pk = proj_fm(wu, wuk, dc * 128, hT, "hT")
                    fc = half * 2 + dc
                    S.op("act", ACT(r_relu[fc % 2], ps[:, 0:TB], AF.Relu), reads=[pk], writes=[f"relu{fc % 2}"])
                    S.op("act", ACT(aT[:, fc, :], r_relu[fc % 2], AF.Square), reads=[f"relu{fc % 2}"], writes=[f"aT{fc}"])
                WREL()
            wd = []
            for half in range(2):
                w, wk = WGET(wdn_d[F * 512 + half * 256:F * 512 + (half + 1) * 256, :], 2, D)
                wd.append((w, wk))
            for t in range(NT):
                for cgi in range(D // 512):
                    ps, pk = PS()
                    S.op("pe", [MM(ps[:, 0:512], aT[:, fc, t * 128:(t + 1) * 128], wd[fc // 2][0][:, fc % 2, cgi * 512:(cgi + 1) * 512],
                                   fc == 0, fc == 3) for fc in range(4)],
                         reads=[f"aT{fc}" for fc in range(4)] + [wd[0][1], wd[1][1]], writes=[pk])
                    sl = acc[:, t, cgi * 512:(cgi + 1) * 512]
                    S.op("dve", TT(sl, ps[:, 0:512], sl, ALU.add), reads=[pk], writes=[f"acc{t}"])
            WREL()
            WREL()
        if at("mlp"):
            return "STOP"
        S.fence()
        for t in range(NT):
            a = acc[:, t, :]
            S.op("act", ACT(xsb[0], a, AF.Square, accum=ss[:, t:t + 1]), reads=[f"acc{t}"], writes=["xsb0", "ssb"])
            rstd_from_ss(ss[:, t:t + 1], D, rstd[:, t:t + 1])
            S.op("dve", STT(a, a, rstd[:, t:t + 1], fgain, ALU.mult, ALU.mult), reads=["rsb", "fgain"], writes=[f"acc{t}"])
            S.dma("sp", out_d[r0 + t * 128:r0 + (t + 1) * 128, :], a, reads=[f"acc{t}"], writes=[f"out{t}"], is_output=True)

    def program():
        st["ps"] = 0
        st["pb"] = 0
        st["wi"] = 0
        st["wrel"] = 0
        cnt["ch"] = 0
        held.clear()
        S.op("pool", lambda e: e.memset(small[:], 0.0), writes=["ssb", "rsb"])
        S.dma("sp", g1t[:], g1t_d, writes=["g1t"])
        S.dma("sp", g2t[:], g2t_d, writes=["g2t"])
        S.dma("sp", cw[:], cw_d, writes=["cw"])
        S.dma("sp", zeta[:], zeta_d, writes=["zeta"])
        S.dma("sp", negA[:], alog_d.partition_broadcast(P), writes=["negA"])
        S.dma("sp", dtb[:], dtb_d.partition_broadcast(P), writes=["dtb"])
        S.dma("sp", AR[:, 0:128], gdnc_d[:, 256:384], writes=["AR"])
        S.op("dve", CP(identb[:], AR[:, 0:128]), reads=["AR"], writes=["identb"])
        S.op("act", ACT(negA[:], negA[:], AF.Exp), reads=["negA"], writes=["negA"])
        S.op("dve", TS(negA[:], negA[:], -1.0, ALU.mult), reads=["negA"], writes=["negA"])
        S.op("pool", lambda e: e.memset(Rst[:], 0.0), writes=[f"R{h}" for h in range(RH)])
        S.op("pool", lambda e: e.memset(Sst[:], 0.0), writes=[f"S{h}" for h in range(GH)])
        S.op("pool", lambda e: e.memset(halo[:], 0.0), writes=[f"halo{c}" for c in range(3 * GH)])
        S.fence(["g1t", "g2t", "cw", "zeta", "negA", "dtb", "identb", "AR"])
        stopped = False
        for bi in range(NB):
            if block(False, bi) == "STOP":
                stopped = True
                break
        for bi in range(NB):
            if stopped or block(True, bi) == "STOP":
                break
        S.finish()

    S.dry = True
    program()
    S.dry = False
    program()
    S.emit()
    es.close()
    nc._dbg_layout = {"g_q": g_q, "g_k": g_k, "g_v": g_v, "g_beta": g_beta, "g_g": g_g, "g_eg": g_eg, "g_bw": g_bw,
                      "g_rhsu": g_rhsu, "g_rhsw": g_rhsw, "g_kdec": g_kdec, "g_G2": g_G2, "g_dm": g_dm, "g_dec": g_dec,
                      "g_attn": g_attn, "g_uw": g_uw,
                      "g_gz": g_gz}
    return nc


def _const_tables(D, TC, j):
    RW = D // 2
    RH = RW // 256
    GH = (D // 2) // 128
    f32 = np.float32
    inv = (f32(ROPE_BASE) ** (-(np.arange(128, dtype=f32)) * f32(2.0 / 256))).astype(f32)
    pos_main = (j * TC + np.arange(TC)).astype(f32)
    pos_pre = (max(j - 1, 0) * TC + np.arange(TC)).astype(f32)
    pos = np.concatenate([pos_pre, pos_main])
    ang = (pos[None, :] * inv[:, None]).astype(f32)
    rope = np.stack([np.cos(ang), np.sin(ang)]).astype(f32)
    idx = np.arange(128)
    c = idx[None, :]
    m = idx[:, None]
    retc = np.zeros((RH, 128, 256), f32)
    zeta = np.zeros((128, RH), f32)
    for h in range(RH):
        lg = np.log1p(-2.0 ** (-5.0 - h))
        same = (c // 64) == (m // 64)
        dt = np.where(same, np.exp(lg * np.abs(c - m)), np.where(c > m, np.exp(lg * (c - m)), 0.0))
        retc[h, :, 0:128] = dt / 16.0
        retc[h, :, 128:256] = np.exp(lg * (idx + 1.0))[None, :]
        zeta[:, h] = np.exp(lg * (127.0 - idx)) / 16.0
    jj = idx[:, None]
    cc = idx[None, :]
    Lc = (jj <= cc).astype(f32)
    Uc = (jj > cc).astype(f32)
    ident = np.eye(128, dtype=f32)
    ones = np.ones((128, 128), f32)
    NEGS = np.where(jj > cc, 0.0, -1e4).astype(f32)
    NEGT = np.where(cc >= jj, 0.0, -1e4).astype(f32)
    gdnc = np.concatenate([Lc, Uc, ident, ones, NEGS, NEGT], axis=1).astype(f32)
    return rope, retc, zeta, gdnc


def _run(x, ln1_gain, w_in, ret_norm_gain, gdn_conv_w, gdn_A_log, gdn_dt_bias, gdn_norm_gain,
         w_out, ln2_gain, w_up, w_down, final_gain, debug=False, stop=False):
    x = np.asarray(x, np.float32)
    B, T, D = x.shape
    DFF = w_up.shape[-1]
    TC = T // 2
    NB = TC // TB
    KC = D // 128
    GH = (D // 2) // 128
    nc = build_program(D, DFF, NB, debug=debug, stop=stop)
    _run.nc = nc
    f = lambda a: np.ascontiguousarray(np.asarray(a, np.float32))
    w_in0, w_out0, w_up0, w_down0 = f(w_in[0]), f(w_out[0]), f(w_up[0]), f(w_down[0])
    g1t = f(np.asarray(ln1_gain[0]).reshape(KC, 128).T)
    g2t = f(np.asarray(ln2_gain[0]).reshape(KC, 128).T)
    cwa = np.asarray(gdn_conv_w[0], np.float32)
    cw = f(cwa.reshape(4, 3 * GH, 128).transpose(2, 0, 1).reshape(128, 12 * GH))
    in_maps = []
    zeros = np.zeros((TC, D), np.float32)
    for c in range(2 * B):
        b, j = divmod(c, 2)
        rope, retc, zeta, gdnc = _const_tables(D, TC, j)
        in_maps.append({
            "xp": f(x[b, 0:TC]) if j == 1 else zeros,
            "xm": f(x[b, j * TC:(j + 1) * TC]),
            "w_in": w_in0, "w_out": w_out0, "w_up": w_up0, "w_down": w_down0,
            "rope": rope, "retc": retc, "zeta": zeta, "gdnc": gdnc,
            "g1t": g1t, "g2t": g2t, "cw": cw,
            "rgain": f(ret_norm_gain[0]), "ggain": f(gdn_norm_gain[0]),
            "alog": f(gdn_A_log[0]), "dtb": f(gdn_dt_bias[0]), "fgain": f(final_gain),
        })
    res = run_bass_kernel_spmd(nc, in_maps, core_ids=list(range(2 * B)))
    out = np.empty((B, T, D), np.float32)
    for c in range(2 * B):
        b, j = divmod(c, 2)
        out[b, j * TC:(j + 1) * TC] = np.asarray(res.results[c]["out"], np.float32)
    if debug:
        return out, [np.asarray(r["dbg"]) for r in res.results]
    return out


def kernel(x, ln1_gain, w_in, ret_norm_gain, gdn_conv_w, gdn_A_log, gdn_dt_bias, gdn_norm_gain,
           w_out, ln2_gain, w_up, w_down, final_gain):
    return _run(x, ln1_gain, w_in, ret_norm_gain, gdn_conv_w, gdn_A_log, gdn_dt_bias, gdn_norm_gain,
                w_out, ln2_gain, w_up, w_down, final_gain)
```

```python
import numpy as np
from contextlib import ExitStack
import concourse.bass as bass
import concourse.mybir as mybir
from concourse.bass_utils import run_bass_kernel_spmd

F32 = mybir.dt.float32
BF16 = mybir.dt.bfloat16
AF = mybir.ActivationFunctionType
ALU = mybir.AluOpType

P = 128
TB = 512
NT = TB // P
EPS = 1e-6
ROPE_BASE = 10000.0
SEM_ROT = 30000


class Sched:
    ENGS = ("pe", "dve", "act", "pool", "sp")

    def __init__(self, nc, es):
        self.nc, self.es = nc, es
        self.streams = {e: [] for e in self.ENGS}
        self.sems = []
        self.cur = {}
        self.cnt = {}
        self.pe_sems = set()
        self.own_sems = {}
        for e in ("pe", "dve", "act", "pool"):
            self._newsem(e)
        self.known = {e: {} for e in self.ENGS}
        self.lastw = {}
        self.readers = {}
        self.pws = {}
        self.dry = False
        self.rings = {}
        for q, n in (("sp", 8), ("pool", 6), ("act", 4)):
            idx = [self._alloc(f"d{q}{i}") for i in range(n)]
            self.rings[q] = {"idx": idx, "tot": [0] * n, "i": 0}
        self.nps = 0
        self.out_tokens = []
        self.last_tok = {}
        self.barrier_tok = None

    def _alloc(self, name):
        self.sems.append(self.es.enter_context(self.nc.semaphore(name)))
        return len(self.sems) - 1

    def _newsem(self, e):
        i = self._alloc(f"s{e}{len(self.sems)}")
        self.own_sems.setdefault(e, set()).add(i)
        self.cur[e] = i
        self.cnt[e] = 0
        if e == "pe":
            self.pe_sems.add(i)

    def _deps(self, reads, writes, pwrites, nobarrier=False):
        deps = []
        if self.barrier_tok and not nobarrier:
            deps.append(self.barrier_tok)
        for k in reads:
            t = self.lastw.get(k)
            if t:
                deps.append(t)
            deps.extend(self.pws.get(k, {}).items())
        for k in list(writes) + list(pwrites):
            t = self.lastw.get(k)
            if t:
                deps.append(t)
            deps.extend(self.readers.get(k, {}).items())
        for k in writes:
            deps.extend(self.pws.get(k, {}).items())
        return deps

    def _commit(self, tok, reads, writes, pwrites):
        s, v = tok
        for k in reads:
            d = self.readers.setdefault(k, {})
            d[s] = max(d.get(s, 0), v)
        for k in writes:
            self.lastw[k] = tok
            self.readers[k] = {}
            self.pws[k] = {}
        for k in pwrites:
            d = self.pws.setdefault(k, {})
            d[s] = max(d.get(s, 0), v)

    def _waits(self, eng, deps):
        need = {}
        kn = self.known[eng]
        for s, v in deps:
            if eng == "pe" and s in self.pe_sems:
                continue
            if kn.get(s, 0) >= v:
                continue
            if need.get(s, 0) < v:
                need[s] = v
        for s, v in need.items():
            kn[s] = v
        return list(need.items())

    def op(self, eng, fns, reads=(), writes=(), pwrites=()):
        if self.dry:
            return
        if callable(fns):
            fns = [fns]
        lk = [k + "_lk" for k in reads if k[:2] in ("ps", "pb")]
        deps = self._deps(reads, writes, pwrites)
        if lk:
            own = self.own_sems[eng]
            deps = deps + [d for d in self._deps((), lk, ()) if d[0] not in own]
            writes = list(writes) + lk
        waits = self._waits(eng, deps)
        if self.cnt[eng] >= SEM_ROT:
            self._newsem(eng)
        self.cnt[eng] += 1
        tok = (self.cur[eng], self.cnt[eng])
        self.last_tok[eng] = tok
        self._commit(tok, reads, writes, pwrites)
        self.streams[eng].append((waits, fns, (tok[0], 1)))

    def dma(self, q, out, in_, reads=(), writes=(), pwrites=(), is_output=False, nobarrier=False):
        if self.dry:
            return
        ring = self.rings[q]
        i = ring["i"]
        ring["i"] = (i + 1) % len(ring["idx"])
        deps = self._deps(reads, writes, pwrites, nobarrier)
        if ring["tot"][i]:
            deps.append((ring["idx"][i], ring["tot"][i]))
        waits = self._waits(q, deps)
        ring["tot"][i] += 16
        tok = (ring["idx"][i], ring["tot"][i])
        self._commit(tok, reads, writes, pwrites)
        self.streams[q].append((waits, [lambda e, o=out, a=in_: e.dma_start(out=o, in_=a)], (tok[0], 16)))
        if is_output:
            self.out_tokens.append(tok)

    def fence(self, keys=None):
        if self.dry:
            return
        deps = [t for t in self.last_tok.values()]
        for ring in self.rings.values():
            for i, tot in zip(ring["idx"], ring["tot"]):
                if tot:
                    deps.append((i, tot))
        ring = self.rings["sp"]
        i = ring["i"]
        ring["i"] = (i + 1) % len(ring["idx"])
        if ring["tot"][i]:
            deps.append((ring["idx"][i], ring["tot"][i]))
        waits = self._waits("sp", deps)
        ring["tot"][i] += 16
        tok = (ring["idx"][i], ring["tot"][i])
        d0, d1 = self.fence_buf
        self.streams["sp"].append((waits, [lambda e: e.dma_start(out=d0, in_=d1)], (tok[0], 16)))
        self.barrier_tok = tok

    def finish(self):
        if self.dry:
            return
        waits = self._waits("sp", self.out_tokens)
        self.streams["sp"].append((waits, None, None))

    def emit(self):
        block = self.es.enter_context(self.nc.Block())
        sems = self.sems

        def mk(stream):
            def body(e):
                for waits, fns, inc in stream:
                    for s, v in waits:
                        e.wait_ge(sems[s], v)
                    if fns is None:
                        continue
                    ins = None
                    for f in fns:
                        ins = f(e)
                    ins.then_inc(sems[inc[0]], inc[1])
            return body

        block.tensor(mk(self.streams["pe"]))
        block.vector(mk(self.streams["dve"]))
        block.scalar(mk(self.streams["act"]))
        block.gpsimd(mk(self.streams["pool"]))
        block.sync(mk(self.streams["sp"]))


def MM(out, lhsT, rhs, start=True, stop=True):
    return lambda e: e.matmul(out, lhsT=lhsT, rhs=rhs, start=start, stop=stop)


def TR(out, in_, ident):
    return lambda e: e.transpose(out, in_, ident)


def ACT(out, in_, func, bias=None, scale=None, accum=None):
    kw = {}
    if bias is not None:
        kw["bias"] = bias
    if scale is not None:
        kw["scale"] = scale
    if accum is not None:
        kw["accum_out"] = accum
    return lambda e: e.activation(out=out, in_=in_, func=func, **kw)


def TT(out, a, b, op):
    return lambda e: e.tensor_tensor(out=out, in0=a, in1=b, op=op)


def TS(out, a, s1, op0, s2=None, op1=None):
    if op1 is None:
        return lambda e: e.tensor_scalar(out=out, in0=a, scalar1=s1, scalar2=None, op0=op0)
    return lambda e: e.tensor_scalar(out=out, in0=a, scalar1=s1, scalar2=s2, op0=op0, op1=op1)


def STT(out, a, s, b, op0, op1):
    return lambda e: e.scalar_tensor_tensor(out=out, in0=a, scalar=s, in1=b, op0=op0, op1=op1)


def CP(out, in_):
    return lambda e: e.tensor_copy(out=out, in_=in_)


def RCP(out, in_):
    return lambda e: e.reciprocal(out=out, in_=in_)


def build_program(D, DFF, NB, debug=False, stop=False):
    KC = D // P
    RW = D // 2
    GW = D // 2
    RH = RW // 256
    GH = GW // 128
    TC = NB * TB
    IN_COLS = 4 * RW + 4 * GW + 2 * GH
    GB = 4 * RW
    NWT = 3
    WEL = KC * 256
    assert (DFF // 256) * 0 == 0

    nc = bass.Bass("TRN2", target_bir_lowering=False)
    din = lambda n, s: nc.dram_tensor(n, s, F32, kind="ExternalInput").ap()
    xp_d = din("xp", [TC, D])
    xm_d = din("xm", [TC, D])
    win_d = din("w_in", [D, IN_COLS])
    wout_d = din("w_out", [D, D])
    wup_d = din("w_up", [D, DFF])
    wdn_d = din("w_down", [DFF, D])
    rope_d = din("rope", [2, P, 2 * TC])
    retc_d = din("retc", [RH, P, 256])
    zeta_d = din("zeta", [P, RH])
    gdnc_d = din("gdnc", [P, 768])
    g1t_d = din("g1t", [P, KC])
    g2t_d = din("g2t", [P, KC])
    cw_d = din("cw", [P, 4 * 3 * GH])
    rgain_d = din("rgain", [RW])
    ggain_d = din("ggain", [128])
    alog_d = din("alog", [GH])
    dtb_d = din("dtb", [GH])
    fgain_d = din("fgain", [D])
    out_d = nc.dram_tensor("out", [TC, D], F32, kind="ExternalOutput").ap()
    dbg_d = nc.dram_tensor("dbg", [D, TC], BF16, kind="ExternalOutput").ap() if debug else None

    es = ExitStack()
    S = Sched(nc, es)
    sb = lambda n, s, d=F32: es.enter_context(nc.sbuf_tensor("sb_" + n, s, d))

    hT_t = sb("hT", [P, KC * TB], BF16)
    hT = hT_t[:].rearrange("p (k t) -> p k t", k=KC)
    FG_OFF = max(D, 2048)
    YR = sb("YR", [P, max(KC * TB // 2, FG_OFF + D)], F32)
    yT = YR[:, 0:KC * TB // 2].bitcast(BF16).rearrange("p (k t) -> p k t", k=KC)
    xsb = [YR[:, i * (D // 2):(i + 1) * (D // 2)].bitcast(BF16) for i in range(2)]
    fgain = YR[:, FG_OFF:FG_OFF + D]
    aT = YR[:, 0:4 * TB // 2].bitcast(BF16).rearrange("p (k t) -> p k t", k=4)
    ARN = max(NT * D, 16384)
    r_relu = [YR[:, 1024 + ii * TB:1024 + (ii + 1) * TB] for ii in range(2)]
    AR = sb("AR", [P, ARN], F32)
    acc = AR[:, 0:NT * D].rearrange("p (t d) -> p t d", t=NT)
    wt_t = [sb(f"wt{i}", [P, WEL], BF16) for i in range(NWT)]
    Rst = sb("Rst", [P, RH * 512], F32)
    Sst = sb("Sst", [P, GH * 128], F32)
    halo = sb("halo", [P, 3 * GH * 3], F32)
    g1t = sb("g1t", [P, KC])
    g2t = sb("g2t", [P, KC])
    cw = sb("cw", [P, 12 * GH])
    zeta = sb("zeta", [P, RH])
    identb = sb("identb", [P, P], BF16)
    small = sb("small", [P, 64])
    S.fence_buf = (small[:, 60:61], small[:, 61:62])
    ss = small[:, 0:8]
    rstd = small[:, 8:16]
    gst = small[:, 16:32]
    negA = sb("negA", [P, GH])
    dtb = sb("dtb", [P, GH])

    class Carver:
        def __init__(self, base, limit):
            self.o, self.limit = base, limit

        def f32(self, n):
            a = AR[:, self.o:self.o + n]
            self.o += n
            assert self.o <= self.limit, (self.o, self.limit)
            return a

        def bf16(self, n):
            assert n % 2 == 0
            a = AR[:, self.o:self.o + n // 2].bitcast(BF16)
            self.o += n // 2
            assert self.o <= self.limit, (self.o, self.limit)
            return a

    cv = Carver(0, ARN)
    gdnc = cv.f32(768)
    Lc, Uc, identf, ones = (gdnc[:, i * 128:(i + 1) * 128] for i in range(4))
    NEG2 = gdnc[:, 512:768]
    ropec = cv.f32(2 * TB).rearrange("p (c t) -> p c t", c=2)
    ggain = cv.f32(128)
    GDN_BASE = cv.o
    rgain = cv.f32(RW)
    CONST_END = cv.o
    xt = [AR[:, CONST_END + i * D:CONST_END + (i + 1) * D] for i in range(2)]
    assert CONST_END + 2 * D <= ARN
    cr = Carver(CONST_END, ARN)
    r_tmp = [cr.f32(TB) for _ in range(4)]
    r_retc = [cr.f32(256) for _ in range(2)]
    r_qr = [cr.bf16(2 * TB).rearrange("p (c t) -> p c t", c=2) for _ in range(2)]
    r_qx = [cr.bf16(2 * TB).rearrange("p (c t) -> p c t", c=2) for _ in range(2)]
    r_kr = [cr.bf16(2 * TB).rearrange("p (c t) -> p c t", c=2) for _ in range(2)]
    r_v = [[cr.bf16(256) for _ in range(NT)] for _ in range(2)]
    r_g1 = [cr.f32(256) for _ in range(NT)]
    r_g = [r_g1, r_g1]
    r_kd = [[cr.bf16(256) for _ in range(NT)] for _ in range(2)]
    r_stm = [cr.bf16(128) for _ in range(2)]
    r_yn = [cr.f32(256) for _ in range(2)]
    r_yb = [cr.bf16(256) for _ in range(2)]
    r_sq = cr.f32(256)
    r_rbf = cr.bf16(512)
    r_st = cr.f32(16)
    RET_KEYS = []
    cg = Carver(GDN_BASE, ARN)
    g_u2 = [cg.f32(TB + 4) for _ in range(2)]
    g_acc2 = [cg.f32(TB) for _ in range(2)]
    g_cs2 = [cg.f32(TB) for _ in range(2)]
    g_u, g_acc, g_cs, g_rs = g_u2[0], g_acc2[0], g_cs2[0], g_u2[1]
    g_q = [cg.f32(TB) for _ in range(2)]
    g_k = [cg.f32(TB) for _ in range(2)]
    g_v = [cg.f32(TB) for _ in range(2)]
    g_gz = [cg.f32(256) for _ in range(NT)]
    g_beta = [cg.f32(GH) for _ in range(NT)]
    g_g = [cg.f32(GH) for _ in range(NT)]
    g_eg = [cg.f32(3 * GH) for _ in range(NT)]
    g_bw = [cg.f32(GH) for _ in range(NT)]
    g_xa_t = [cg.f32(GH) for _ in range(NT)]
    g_e1_t = [cg.f32(GH) for _ in range(NT)]
    NBUF = 4
    g_rhsu = [cg.f32(128) for _ in range(NBUF)]
    g_rhsw = [cg.f32(128) for _ in range(NBUF)]
    g_kdec = [cg.f32(128) for _ in range(NBUF)]
    g_G2 = [cg.f32(128) for _ in range(NBUF)]
    g_dec = [cg.f32(256) for _ in range(NBUF)]
    g_dm = g_dec
    g_L = [cg.f32(128) for _ in range(NBUF)]
    g_UP = [cg.f32(256) for _ in range(NBUF)]
    g_attn = [cg.f32(128) for _ in range(NBUF)]
    g_uw = [cg.f32(256) for _ in range(NBUF)]
    g_vnew, g_a1, g_o = g_G2, g_rhsu, g_rhsw
    g_junk2 = [g_dec[ii][:, 0:128] for ii in range(NBUF)]
    g_yb = [cg.bf16(128) for _ in range(NBUF)]
    g_st = cg.f32(2 * NBUF)

    NPS = 6
    psf = [es.enter_context(nc.psum_tensor(f"psf{i}", [P, 512], F32)) for i in range(NPS)]
    psb = [es.enter_context(nc.psum_tensor(f"psb{i}", [P, 1024], BF16)) for i in range(2)]
    st = {"ps": 0, "pb": 0, "wi": 0, "wrel": 0}

    held = {}

    def PS(hold=False):
        for _ in range(NPS):
            i = st["ps"]
            st["ps"] = (i + 1) % NPS
            if not held.get(f"ps{i}"):
                if hold:
                    held[f"ps{i}"] = True
                return psf[i], f"ps{i}"
        assert not hold, "no free PSUM bank"
        return None, None

    def PB(hold=False):
        for _ in range(2):
            i = st["pb"]
            st["pb"] = 1 - i
            if not held.get(f"pb{i}"):
                if hold:
                    held[f"pb{i}"] = True
                return psb[i], f"pb{i}"
        return None, None

    def PREL(k):
        held[k] = False

    wplan = []

    def WGET(src, kc, cols):
        n = st["wi"]
        st["wi"] = n + 1
        buf = wt_t[n % NWT]
        view = buf[:, 0:kc * cols].rearrange("p (k c) -> p k c", k=kc)
        if S.dry:
            wplan.append((src, kc, cols))
            return view, f"w{n % NWT}"
        assert wplan[n][1:] == (kc, cols)
        assert n < st["wrel"] + NWT, "too many live weight tiles"
        if n == 0:
            for j in range(min(NWT, len(wplan))):
                _wload(j)
        return view, f"w{n % NWT}"

    def WREL():
        m = st["wrel"]
        st["wrel"] = m + 1
        if S.dry:
            return
        if m + NWT < len(wplan):
            _wload(m + NWT)

    def _wload(j):
        src, kc, cols = wplan[j]
        buf = wt_t[j % NWT]
        dst = buf[:, 0:kc * cols].rearrange("p (k c) -> p k c", k=kc)
        S.dma("pool", dst, src.rearrange("(k p) c -> p k c", p=P), writes=[f"w{j % NWT}"], nobarrier=True)

    def rstd_from_ss(ss_ap, n, dst):
        S.op("act", ACT(dst, ss_ap, AF.Sqrt, bias=EPS, scale=1.0 / n), reads=["ssb"], writes=["rsb"])
        S.op("dve", RCP(dst, dst), reads=["rsb"], writes=["rsb"])

    def hT_tile(t, xa, skey, gT, dst, dst_key):
        xb = xsb[t % 2]
        xk = f"xsb{t % 2}"
        S.op("act", ACT(xb, xa, AF.Square, accum=ss[:, t:t + 1]), reads=[skey], writes=[xk, f"ssb{t}"])
        yield
        S.op("act", ACT(rstd[:, t:t + 1], ss[:, t:t + 1], AF.Sqrt, bias=EPS, scale=1.0 / D), reads=[f"ssb{t}"], writes=[f"rsb{t}"])
        yield
        S.op("dve", RCP(rstd[:, t:t + 1], rstd[:, t:t + 1]), reads=[f"rsb{t}"], writes=[f"rsb{t}"])
        yield
        S.op("dve", TS(xb, xa, rstd[:, t:t + 1], ALU.mult), reads=[skey, f"rsb{t}"], writes=[xk])
        yield
        for k0 in range(0, KC, 4):
            pb, pk = PB(True)
            while pb is None:
                yield
                pb, pk = PB(True)
            S.op("pe", [TR(pb[:, j * 128:(j + 1) * 128], xb[:, (k0 + j) * 128:(k0 + j + 1) * 128], identb[:])
                        for j in range(4)], reads=[xk], writes=[pk])
            yield
            if (k0 // 4) % 2 == 1:
                for j in range(4):
                    S.op("act", ACT(dst[:, k0 + j, t * 128:(t + 1) * 128], pb[:, j * 128:(j + 1) * 128], AF.Copy,
                                    scale=gT[:, k0 + j:k0 + j + 1]), reads=[pk], pwrites=[dst_key])
            else:
                S.op("dve", TT(dst[:, k0:k0 + 4, t * 128:(t + 1) * 128],
                               pb[:, 0:512].rearrange("p (k t) -> p k t", k=4),
                               gT[:, k0:k0 + 4].unsqueeze(2).to_broadcast([P, 4, 128]), ALU.mult),
                     reads=[pk], pwrites=[dst_key])
            PREL(pk)
            yield

    def make_hT(src_tile_fn, src_keys, gT, dst, dst_key, load=None):
        for t0 in range(0, NT, 2):
            if load is not None:
                load(t0)
                load(t0 + 1)
            lockstep([hT_tile(t, src_tile_fn(t), src_keys[t], gT, dst, dst_key) for t in (t0, t0 + 1)])

    def proj_fm(w, wk, c0, src, src_key, hold=False):
        ps, pk = PS(hold)
        S.op("pe", [MM(ps[:, 0:TB], w[:, kc, c0:c0 + 128], src[:, kc, :], kc == 0, kc == KC - 1) for kc in range(KC)],
             reads=[wk, src_key], writes=[pk])
        return ps, pk

    def proj_tm(w, wk, t, ncols, src, src_key, c0=0):
        ps, pk = PS()
        S.op("pe", [MM(ps[:, 0:ncols], src[:, kc, t * 128:(t + 1) * 128], w[:, kc, c0:c0 + ncols], kc == 0, kc == KC - 1)
                    for kc in range(KC)], reads=[wk, src_key], writes=[pk])
        return ps, pk

    def ret_head(main, h, tok_off):
        b = h % 2
        kq, kx, kk = f"rqr{b}", f"rqx{b}", f"rkr{b}"
        retc = r_retc[b]
        S.dma("sp", retc, retc_d[h], writes=[f"retc{b}"])
        DT = retc[:, 0:128]
        XI = retc[:, 128:256]
        cos = ropec[:, 0, :]
        sin = ropec[:, 1, :]

        def rope(wcol0, dst, dkey, with_xi):
            w, wk = WGET(win_d[:, wcol0:wcol0 + 256], KC, 256)
            p1, k1 = proj_fm(w, wk, 0, hT, "hT")
            p2, k2 = proj_fm(w, wk, 128, hT, "hT")
            WREL()
            t1, t2, t3, t4 = r_tmp
            S.op("dve", TT(t1, p1[:, 0:TB], cos, ALU.mult), reads=[k1, "ropec"], writes=["rt1"])
            S.op("dve", TT(t2, p2[:, 0:TB], sin, ALU.mult), reads=[k2, "ropec"], writes=["rt2"])
            S.op("dve", TT(t3, p1[:, 0:TB], sin, ALU.mult), reads=[k1, "ropec"], writes=["rt3"])
            S.op("dve", TT(t4, p2[:, 0:TB], cos, ALU.mult), reads=[k2, "ropec"], writes=["rt4"])
            if not with_xi:
                S.op("pool", TT(dst[:, 0, :], t1, t2, ALU.subtract), reads=["rt1", "rt2"], pwrites=[dkey])
                S.op("pool", TT(dst[:, 1, :], t3, t4, ALU.add), reads=["rt3", "rt4"], pwrites=[dkey])
                return
            xi_b = XI.unsqueeze(1).to_broadcast([P, NT, 128])
            v3 = lambda a: a.rearrange("p (t c) -> p t c", t=NT)
            S.op("pool", TT(t1, t1, t2, ALU.subtract), reads=["rt1", "rt2"], writes=["rt1"])
            S.op("pool", TT(t3, t3, t4, ALU.add), reads=["rt3", "rt4"], writes=["rt3"])
            S.op("act", ACT(dst[:, 0, :], t1, AF.Copy), reads=["rt1"], pwrites=[dkey])
            S.op("act", ACT(dst[:, 1, :], t3, AF.Copy), reads=["rt3"], pwrites=[dkey])
            S.op("pool", TT(v3(r_qx[b][:, 0, :]), v3(t1), xi_b, ALU.mult), reads=["rt1", f"retc{b}"], pwrites=[kx])
            S.op("pool", TT(v3(r_qx[b][:, 1, :]), v3(t3), xi_b, ALU.mult), reads=["rt3", f"retc{b}"], pwrites=[kx])

        if main:
            rope(h * 256, r_qr[b], kq, True)
        rope(RW + h * 256, r_kr[b], kk, False)
        wv, wvk = WGET(win_d[:, 2 * RW + h * 256:2 * RW + (h + 1) * 256], KC, 256)
        for t in range(NT):
            ps, pk = proj_tm(wv, wvk, t, 256, hT, "hT")
            S.op("act", ACT(r_v[b][t], ps[:, 0:256], AF.Copy), reads=[pk], writes=[f"rv{b}{t}"])
        WREL()
        if main:
            wg, wgk = WGET(win_d[:, 3 * RW + h * 256:3 * RW + (h + 1) * 256], KC, 256)
            for t in range(NT):
                ps, pk = proj_tm(wg, wgk, t, 256, hT, "hT")
                S.op("act", ACT(r_g[b][t], ps[:, 0:256], AF.Silu), reads=[pk], writes=[f"rg{t}"])
                S.op("pool", TT(r_g[b][t], r_g[b][t], rgain[:, h * 256:(h + 1) * 256], ALU.mult),
                     reads=[f"rg{t}", "rgain"], writes=[f"rg{t}"])
            WREL()
        for t in range(NT):
            pb, pk = PB()
            S.op("pe", [TR(pb[:, dc * 128:(dc + 1) * 128], r_kr[b][:, dc, t * 128:(t + 1) * 128], identb[:]) for dc in range(2)],
                 reads=[kk], writes=[pk])
            S.op("act", ACT(r_kd[b][t], pb[:, 0:256], AF.Copy, scale=zeta[:, h:h + 1]), reads=[pk], writes=[f"rkd{b}{t}"])
        R = Rst[:, h * 512:(h + 1) * 512]
        Rk = f"R{h}"
        g128 = float(np.exp(np.log1p(-2.0 ** (-5.0 - h)) * 128.0))
        S.op("act", ACT(r_rbf, R, AF.Copy), reads=[Rk], writes=["rbf"])
        for t in range(NT):
            tsl = slice(t * 128, (t + 1) * 128)
            if main:
                ps_s, ks = PS()
                S.op("pe", [MM(ps_s[:, 0:128], r_kr[b][:, dc, tsl], r_qr[b][:, dc, tsl], dc == 0, dc == 1) for dc in range(2)],
                     reads=[kk, kq], writes=[ks])
                stm = r_stm[t % 2]
                S.op("dve", TT(stm, ps_s[:, 0:128], DT, ALU.mult), reads=[ks, f"retc{b}"], writes=[f"stm{t % 2}"])
                ps_o, ko = PS()
                S.op("pe", [MM(ps_o[:, 0:256], stm, r_v[b][t], True, False),
                            MM(ps_o[:, 0:256], r_qx[b][:, 0, tsl], r_rbf[:, 0:256], False, False),
                            MM(ps_o[:, 0:256], r_qx[b][:, 1, tsl], r_rbf[:, 256:512], False, True)],
                     reads=[f"stm{t % 2}", f"rv{b}{t}", kx, "rbf"], writes=[ko])
            ps_r, kr_ = PS()
            S.op("pe", [MM(ps_r[:, dc * 256:(dc + 1) * 256], r_kd[b][t][:, dc * 128:(dc + 1) * 128], r_v[b][t], True, True)
                        for dc in range(2)], reads=[f"rkd{b}{t}", f"rv{b}{t}"], writes=[kr_])
            S.op("dve", STT(R, R, g128, ps_r[:, 0:512], ALU.mult, ALU.add), reads=[kr_, Rk], writes=[Rk])
            S.op("act", ACT(r_rbf, R, AF.Copy), reads=[Rk], writes=["rbf"])
            if main:
                yn = r_yn[t % 2]
                ynk = f"ryn{t % 2}"
                S.op("act", ACT(yn, ps_o[:, 0:256], AF.Identity, accum=r_st[:, 0:1]), reads=[ko], writes=[ynk, "rst0"])
                S.op("act", ACT(r_sq, ps_o[:, 0:256], AF.Square, accum=r_st[:, 1:2]), reads=[ko], writes=["rsq", "rst1"])
                S.op("dve", TS(r_st[:, 2:3], r_st[:, 0:1], 1.0 / 256, ALU.mult), reads=["rst0"], writes=["rst2"])
                S.op("dve", TT(r_st[:, 3:4], r_st[:, 2:3], r_st[:, 2:3], ALU.mult), reads=["rst2"], writes=["rst3"])
                S.op("dve", STT(r_st[:, 4:5], r_st[:, 1:2], 1.0 / 256, r_st[:, 3:4], ALU.mult, ALU.subtract),
                     reads=["rst1", "rst3"], writes=["rst4"])
                S.op("act", ACT(r_st[:, 4:5], r_st[:, 4:5], AF.Sqrt, bias=EPS), reads=["rst4"], writes=["rst4"])
                S.op("dve", RCP(r_st[:, 4:5], r_st[:, 4:5]), reads=["rst4"], writes=["rst4"])
                S.op("dve", TS(yn, yn, r_st[:, 2:3], ALU.subtract, r_st[:, 4:5], ALU.mult), reads=[ynk, "rst2", "rst4"], writes=[ynk])
                yb = r_yb[t % 2]
                S.op("pool", TT(yb, yn, r_g[b][t], ALU.mult), reads=[ynk, f"rg{t}"], writes=[f"ryb{t % 2}"])
                pb, pk = PB()
                S.op("pe", [TR(pb[:, dc * 128:(dc + 1) * 128], yb[:, dc * 128:(dc + 1) * 128], identb[:]) for dc in range(2)],
                     reads=[f"ryb{t % 2}"], writes=[pk])
                S.op("act", ACT(yT[:, 2 * h:2 * h + 2, tsl], pb[:, 0:256].rearrange("p (k t) -> p k t", k=2), AF.Copy),
                     reads=[pk], pwrites=["yT"])

    def gdn_gates(tok_off):
        w, wk = WGET(win_d[:, GB + 4 * GW:GB + 4 * GW + 2 * GH], KC, 2 * GH)
        for t in range(NT):
            g_xa, g_e1 = g_xa_t[t], g_e1_t[t]
            ps, pk = proj_tm(w, wk, t, 2 * GH, hT, "hT")
            S.op("act", ACT(g_beta[t], ps[:, 0:GH], AF.Sigmoid), reads=[pk], writes=[f"gbeta{t}"])
            S.op("dve", TT(g_xa, ps[:, GH:2 * GH], dtb[:], ALU.add), reads=[pk], writes=[f"gxa{t}"])
            S.op("act", ACT(g_e1, g_xa, AF.Abs), reads=[f"gxa{t}"], writes=[f"ge1{t}"])
            S.op("act", ACT(g_e1, g_e1, AF.Exp, scale=-1.0), reads=[f"ge1{t}"], writes=[f"ge1{t}"])
            S.op("act", ACT(g_e1, g_e1, AF.Ln, bias=1.0), reads=[f"ge1{t}"], writes=[f"ge1{t}"])
            S.op("dve", STT(g_xa, g_xa, 0.0, g_e1, ALU.max, ALU.add), reads=[f"gxa{t}", f"ge1{t}"], writes=[f"gxa{t}"])
            S.op("dve", TT(g_g[t], g_xa, negA[:], ALU.mult), reads=[f"gxa{t}"], writes=[f"gg{t}"])
            ps2, pk2 = PS()
            S.op("pe", [MM(ps2[:, 0:GH], Lc, g_g[t]), MM(ps2[:, GH:2 * GH], Uc, g_g[t]), MM(ps2[:, 2 * GH:3 * GH], ones, g_g[t])],
                 reads=[f"gg{t}", "gdnc"], writes=[pk2])
            S.op("act", ACT(g_eg[t], ps2[:, 0:3 * GH], AF.Exp), reads=[pk2], writes=[f"geg{t}"])
            S.op("dve", TT(g_bw[t], g_beta[t], g_eg[t][:, 0:GH], ALU.mult), reads=[f"gbeta{t}", f"geg{t}"], writes=[f"gbw{t}"])
        WREL()

    def gdn_feat(w, wk, j, chunk, dst, dkey, mode, fi):
        gu, gacc, gcs = g_u2[fi], g_acc2[fi], g_cs2[fi]
        ku, ka, kc_ = f"gu{fi}", f"gacc{fi}", f"gcs{fi}"
        ps, pk = proj_fm(w, wk, j * 128, hT, "hT", hold=True)
        hl = halo[:, chunk * 3:(chunk + 1) * 3]
        hk = f"halo{chunk}"
        yield
        if mode == "halo":
            S.op("act", ACT(hl, ps[:, TB - 3:TB], AF.Copy), reads=[pk], writes=[hk])
            PREL(pk)
            return
        S.op("act", ACT(gu[:, 3:3 + TB], ps[:, 0:TB], AF.Copy), reads=[pk], writes=[ku])
        PREL(pk)
        S.op("act", ACT(gu[:, 0:3], hl, AF.Copy), reads=[hk], pwrites=[ku])
        yield
        S.op("act", ACT(hl, gu[:, TB:TB + 3], AF.Copy), reads=[ku], writes=[hk])
        cwc = lambda jj: cw[:, jj * 3 * GH + chunk:jj * 3 * GH + chunk + 1]
        S.op("act", ACT(gacc, gu[:, 3:3 + TB], AF.Copy, scale=cwc(3)), reads=[ku], writes=[ka])
        yield
        for jj in (2, 1, 0):
            S.op("dve", STT(gacc, gu[:, jj:jj + TB], cwc(jj), gacc, ALU.mult, ALU.add), reads=[ku, ka], writes=[ka])
            yield
        if mode == "v":
            S.op("act", ACT(dst, gacc, AF.Silu), reads=[ka], writes=[dkey])
            return
        S.op("act", ACT(gcs, gacc, AF.Silu), reads=[ka], writes=[kc_])
        yield
        gsq = gacc
        S.op("pool", TT(gsq, gcs, gcs, ALU.mult), reads=[kc_], writes=[ka])
        yield
        ps2, pk2 = PS(True)
        S.op("pe", [MM(ps2[:, hh * 256:(hh + 1) * 256], ones, gsq[:, hh * 256:(hh + 1) * 256]) for hh in range(TB // 256)],
             reads=[ka, "gdnc"], writes=[pk2])
        yield
        grs = gu[:, 0:TB]
        S.op("act", ACT(grs, ps2[:, 0:TB], AF.Sqrt, bias=EPS), reads=[pk2], writes=[ku])
        PREL(pk2)
        yield
        S.op("dve", RCP(grs, grs), reads=[ku], writes=[ku])
        yield
        if mode == "q":
            S.op("dve", STT(dst, gcs, float(128 ** -0.5), grs, ALU.mult, ALU.mult), reads=[kc_, ku], writes=[dkey])
        else:
            S.op("pool", TT(dst, gcs, grs, ALU.mult), reads=[kc_, ku], writes=[dkey])

    cnt = {"ch": 0}

    def gdn_chunk(main, hd, t, b, i):
        tsl = slice(t * 128, (t + 1) * 128)
        hs = slice(hd, hd + 1)
        kT, qT, vT = g_k[b], g_q[b], g_v[b]
        K = lambda n: f"{n}{i}"
        ps, pk = PS(True)
        S.op("pe", [TR(ps[:, 0:128], kT[:, tsl], identf), TR(ps[:, 128:256], vT[:, tsl], identf)],
             reads=[f"gk{b}", f"gv{b}", "gdnc"], writes=[pk])
        S.op("act", ACT(g_G2[i], Uc, AF.Copy, scale=g_g[t][:, hs]), reads=["gdnc", f"gg{t}"], writes=[K("G2")])
        yield
        S.op("act", ACT(g_rhsw[i], ps[:, 0:128], AF.Copy, scale=g_bw[t][:, hs]), reads=[pk, f"gbw{t}"], writes=[K("rhsw")])
        S.op("act", ACT(g_kdec[i], ps[:, 0:128], AF.Copy, scale=g_eg[t][:, GH + hd:GH + hd + 1]), reads=[pk, f"geg{t}"], writes=[K("kdec")])
        S.op("act", ACT(g_rhsu[i], ps[:, 128:256], AF.Copy, scale=g_beta[t][:, hs]), reads=[pk, f"gbeta{t}"], writes=[K("rhsu")])
        PREL(pk)
        psD, kD = PS(True)
        mms = [MM(psD[:, 0:128], Lc, g_G2[i]), MM(psD[:, 128:256], g_G2[i], Lc), MM(psD[:, 256:384], kT[:, tsl], kT[:, tsl])]
        if main:
            mms.append(MM(psD[:, 384:512], kT[:, tsl], qT[:, tsl]))
        S.op("pe", mms, reads=[K("G2"), "gdnc", f"gk{b}"] + ([f"gq{b}"] if main else []), writes=[kD])
        yield
        S.op("dve", STT(g_dm[i], psD[:, 0:256], 0.0, NEG2, ALU.min, ALU.add), reads=[kD, "gdnc"], writes=[K("dec")])
        yield
        S.op("act", ACT(g_dec[i], g_dm[i], AF.Exp), reads=[K("dec")], writes=[K("dec")])
        yield
        L, UP = g_L[i], g_UP[i]
        S.op("dve", STT(L, psD[:, 256:384], g_beta[t][:, hs], g_dec[i][:, 0:128], ALU.mult, ALU.mult),
             reads=[kD, K("dec"), f"gbeta{t}"], writes=[K("L")])
        if main:
            S.op("dve", TT(g_attn[i], psD[:, 384:512], g_dec[i][:, 128:256], ALU.mult), reads=[kD, K("dec")], writes=[K("attn")])
        PREL(kD)
        yield
        psU, kU = PS(True)
        S.op("pe", TR(psU[:, 0:128], L, identf), reads=[K("L"), "gdnc"], writes=[kU])
        yield
        S.op("act", ACT(UP[:, 0:128], psU[:, 0:128], AF.Copy), reads=[kU], pwrites=[K("UP")])
        S.op("dve", STT(UP[:, 128:256], psU[:, 0:128], -1.0, identf, ALU.mult, ALU.add), reads=[kU, "gdnc"], pwrites=[K("UP")])
        PREL(kU)
        yield
        Pm = UP[:, 128:256]
        for s_ in range(6):
            psA, kA = PS(True)
            S.op("pe", [MM(psA[:, 0:256], L, UP), MM(psA[:, 256:384], UP[:, 0:128], L)], reads=[K("L"), K("UP")], writes=[kA])
            yield
            S.op("act", ACT(UP[:, 0:128], psA[:, 0:128], AF.Copy), reads=[kA], pwrites=[K("UP")])
            S.op("act", ACT(L, psA[:, 256:384], AF.Copy), reads=[kA], writes=[K("L")])
            if s_ > 0:
                S.op("dve", TT(Pm, psA[:, 128:256], Pm, ALU.add), reads=[kA], pwrites=[K("UP")])
            PREL(kA)
            yield
        psA, kA = PS(True)
        S.op("pe", MM(psA[:, 0:128], L, Pm), reads=[K("L"), K("UP")], writes=[kA])
        yield
        S.op("dve", TT(Pm, psA[:, 0:128], Pm, ALU.add), reads=[kA, K("UP")], writes=[K("UP")])
        PREL(kA)
        yield
        psu, ku = PS(True)
        S.op("pe", [MM(psu[:, 0:128], Pm, g_rhsu[i]), MM(psu[:, 128:256], g_rhsw[i], Pm)],
             reads=[K("UP"), K("rhsu"), K("rhsw")], writes=[ku])
        yield
        S.op("act", ACT(g_uw[i], psu[:, 0:256], AF.Copy), reads=[ku], writes=[K("uw")])
        PREL(ku)
        yield
        while t > 0 and not rec_done.get((hd, t - 1)):
            yield
        Sm = Sst[:, hd * 128:(hd + 1) * 128]
        Sk = f"S{hd}"
        psS, kS = PS(True)
        mms = [MM(psS[:, 0:128], g_uw[i][:, 128:256], Sm)]
        if main:
            mms.append(MM(psS[:, 128:256], qT[:, tsl], Sm))
        S.op("pe", mms, reads=[K("uw"), Sk] + ([f"gq{b}"] if main else []), writes=[kS])
        yield
        S.op("dve", STT(g_vnew[i], psS[:, 0:128], -1.0, g_uw[i][:, 0:128], ALU.mult, ALU.add), reads=[K("uw"), kS], writes=[K("G2")])
        if main:
            S.op("act", ACT(g_a1[i], psS[:, 128:256], AF.Copy, scale=g_eg[t][:, hs]), reads=[kS, f"geg{t}"], writes=[K("rhsu")])
        PREL(kS)
        yield
        psO, kO = PS(True)
        mms = [MM(psO[:, 128:256], g_kdec[i], g_vnew[i])]
        if main:
            mms.append(MM(psO[:, 0:128], g_attn[i], g_vnew[i]))
        S.op("pe", mms, reads=[K("kdec"), K("G2")] + ([K("attn")] if main else []), writes=[kO])
        yield
        S.op("dve", STT(Sm, Sm, g_eg[t][:, 2 * GH + hd:2 * GH + hd + 1], psO[:, 128:256], ALU.mult, ALU.add),
             reads=[kO, Sk, f"geg{t}"], writes=[Sk])
        rec_done[(hd, t)] = True
        if not main:
            PREL(kO)
        if main:
            S.op("dve", TT(g_o[i], psO[:, 0:128], g_a1[i], ALU.add), reads=[kO, K("rhsu")], writes=[K("rhsw")])
            PREL(kO)
            yield
            S.op("act", ACT(g_junk2[i], g_o[i], AF.Square, accum=g_st[:, 2 * i:2 * i + 1]), reads=[K("rhsw")], writes=[K("dec"), K("gst0")])
            yield
            r1 = g_st[:, 2 * i + 1:2 * i + 2]
            S.op("act", ACT(r1, g_st[:, 2 * i:2 * i + 1], AF.Sqrt, bias=EPS, scale=1.0 / 128), reads=[K("gst0")], writes=[K("gst1")])
            yield
            S.op("dve", RCP(r1, r1), reads=[K("gst1")], writes=[K("gst1")])
            jj = hd % 2
            S.op("dve", STT(g_yb[i], g_o[i], r1, g_gz[t][:, jj * 128:(jj + 1) * 128], ALU.mult, ALU.mult),
                 reads=[K("rhsw"), K("gst1"), f"ggz{t}"], writes=[K("yb")])
            yield
            pb, pk = PB(True)
            while pb is None:
                yield
                pb, pk = PB(True)
            S.op("pe", TR(pb[:, 0:128], g_yb[i], identb[:]), reads=[K("yb")], writes=[pk])
            yield
            S.op("act", ACT(yT[:, RW // 128 + hd, tsl], pb[:, 0:128], AF.Copy), reads=[pk], pwrites=["yT"])
            PREL(pk)

    rec_done = {}

    def lockstep(gens):
        gens = list(gens)
        while gens:
            for g in list(gens):
                try:
                    next(g)
                except StopIteration:
                    gens.remove(g)

    def gdn_pair(main, pair, last_prefix):
        c0 = pair * 256
        if main:
            wz, wzk = WGET(win_d[:, GB + 3 * GW + c0:GB + 3 * GW + c0 + 256], KC, 256)
            for t in range(NT):
                ps, pk = proj_tm(wz, wzk, t, 256, hT, "hT")
                S.op("act", ACT(g_gz[t], ps[:, 0:256], AF.Silu), reads=[pk], writes=[f"ggz{t}"])
                S.op("pool", TT(g_gz[t].rearrange("p (j c) -> p j c", j=2), g_gz[t].rearrange("p (j c) -> p j c", j=2),
                                ggain.unsqueeze(1).to_broadcast([P, 2, 128]), ALU.mult), reads=[f"ggz{t}", "ggain"], writes=[f"ggz{t}"])
            WREL()
        if main or last_prefix:
            wq, wqk = WGET(win_d[:, GB + c0:GB + c0 + 256], KC, 256)
            lockstep([gdn_feat(wq, wqk, j, pair * 2 + j, g_q[j] if main else None, f"gq{j}", "q" if main else "halo", j)
                      for j in range(2)])
            WREL()
        wk_, wkk = WGET(win_d[:, GB + GW + c0:GB + GW + c0 + 256], KC, 256)
        lockstep([gdn_feat(wk_, wkk, j, GH + pair * 2 + j, g_k[j], f"gk{j}", "k", j) for j in range(2)])
        WREL()
        wv, wvk = WGET(win_d[:, GB + 2 * GW + c0:GB + 2 * GW + c0 + 256], KC, 256)
        lockstep([gdn_feat(wv, wvk, j, 2 * GH + pair * 2 + j, g_v[j], f"gv{j}", "v", j) for j in range(2)])
        WREL()
        rec_done.clear()
        for t0 in range(0, NT, 2):
            lockstep([gdn_chunk(main, pair * 2 + j, t, j, 2 * (t % 2) + j) for t in (t0, t0 + 1) for j in range(2)])

    ALLW = [f"w{i}" for i in range(NWT)]

    def load_block_consts(tok_off_rope):
        S.dma("sp", gdnc, gdnc_d, writes=["gdnc"])
        S.dma("sp", ropec, rope_d[:, :, tok_off_rope:tok_off_rope + TB].rearrange("c p t -> p c t"), writes=["ropec"])
        S.dma("sp", rgain, rgain_d.partition_broadcast(P), writes=["rgain"])
        S.dma("sp", ggain, ggain_d.partition_broadcast(P), writes=["ggain"])

    SCR = ["AR"]

    def block(main, bi):
        x_d = xm_d if main else xp_d
        r0 = bi * TB
        at = lambda stage: isinstance(stop, tuple) and stop == (main, bi, stage)
        S.fence(["AR", "acc0", "acc1", "acc2", "acc3", "xt0", "xt1"])
        load_block_consts((TC if main else 0) + r0)
        def load_x(t):
            S.dma("sp", xt[t % 2], x_d[r0 + t * 128:r0 + (t + 1) * 128, :], writes=[f"xt{t % 2}"])

        make_hT(lambda t: xt[t % 2], [f"xt{t % 2}" for t in range(NT)], g1t, hT, "hT", load_x)
        if at("hT"):
            return "STOP"
        S.fence(["xt0", "xt1", "AR"])
        for h in range(RH):
            ret_head(main, h, r0)
        if at("ret"):
            return "STOP"
        S.fence(["AR", "rt1", "rt2", "rt3", "rt4"])
        gdn_gates(r0)
        if at("gates"):
            return "STOP"
        last_prefix = (not main) and bi == NB - 1
        for pair in range(GH // 2):
            if gdn_pair(main, pair, last_prefix) == "STOP":
                return "STOP"
        if at("gdn"):
            return "STOP"
        if not main:
            return
        if stop == True or at("gdn"):
            return "STOP"
        S.fence(["AR"] + [f"acc{t}" for t in range(NT)])
        for t in range(NT):
            S.dma("sp", acc[:, t, :], x_d[r0 + t * 128:r0 + (t + 1) * 128, :], writes=[f"acc{t}"])
        if debug:
            S.dma("sp", dbg_d.rearrange("(k p) t -> p k t", p=P)[:, :, r0:r0 + TB], yT, reads=["yT"], writes=["dbgout"], is_output=True)
        for cgi in range(D // 256):
            w, wk = WGET(wout_d[:, cgi * 256:(cgi + 1) * 256], KC, 256)
            for t in range(NT):
                ps, pk = proj_tm(w, wk, t, 256, yT, "yT")
                sl = acc[:, t, cgi * 256:(cgi + 1) * 256]
                S.op("dve", TT(sl, ps[:, 0:256], sl, ALU.add), reads=[pk], writes=[f"acc{t}"])
            WREL()
        if at("outproj"):
            return "STOP"
        S.fence(["yT", "xsb0", "xsb1", "hT", "dbgout"])
        make_hT(lambda t: acc[:, t, :], [f"acc{t}" for t in range(NT)], g2t, hT, "hT")
        S.fence(["xsb0", "xsb1", "aT", "hT"])
        S.dma("sp", fgain, fgain_d.partition_broadcast(P), writes=["fgain"])
        for F in range(DFF // 512):
            for half in range(2):
                wu, wuk = WGET(wup_d[:, F * 512 + half * 256:F * 512 + (half + 1) * 256], KC, 256)
                for dc in range(2):
                    ps, pk = proj_fm(wu, wuk, dc * 128, hT, "hT")
                    fc = half * 2 + dc
                    S.op("act", ACT(r_relu[fc % 2], ps[:, 0:TB], AF.Relu), reads=[pk], writes=[f"relu{fc % 2}"])
                    S.op("act", ACT(aT[:, fc, :], r_relu[fc % 2], AF.Square), reads=[f"relu{fc % 2}"], writes=[f"aT{fc}"])
                WREL()
            wd = []
            for half in range(2):
                w, wk = WGET(wdn_d[F * 512 + half * 256:F * 512 + (half + 1) * 256, :], 2, D)
                wd.append((w, wk))
            for t in range(NT):
                for cgi in range(D // 512):
                    ps, pk = PS()
                    S.op("pe", [MM(ps[:, 0:512], aT[:, fc, t * 128:(t + 1) * 128], wd[fc // 2][0][:, fc % 2, cgi * 512:(cgi + 1) * 512],
                                   fc == 0, fc == 3) for fc in range(4)],
                         reads=[f"aT{fc}" for fc in range(4)] + [wd[0][1], wd[1][1]], writes=[pk])
                    sl = acc[:, t, cgi * 512:(cgi + 1) * 512]
                    S.op("dve", TT(sl, ps[:, 0:512], sl, ALU.add), reads=[pk], writes=[f"acc{t}"])
            WREL()
            WREL()
        if at("mlp"):
            return "STOP"
        S.fence()
        for t in range(NT):
            a = acc[:, t, :]
            S.op("act", ACT(xsb[t % 2], a, AF.Square, accum=ss[:, t:t + 1]), reads=[f"acc{t}"], writes=[f"xsb{t % 2}", f"ssb{t}"])
            S.op("act", ACT(rstd[:, t:t + 1], ss[:, t:t + 1], AF.Sqrt, bias=EPS, scale=1.0 / D), reads=[f"ssb{t}"], writes=[f"rsb{t}"])
            S.op("dve", RCP(rstd[:, t:t + 1], rstd[:, t:t + 1]), reads=[f"rsb{t}"], writes=[f"rsb{t}"])
            S.op("dve", STT(a, a, rstd[:, t:t + 1], fgain, ALU.mult, ALU.mult), reads=[f"rsb{t}", "fgain"], writes=[f"acc{t}"])
            S.dma("sp", out_d[r0 + t * 128:r0 + (t + 1) * 128, :], a, reads=[f"acc{t}"], writes=[f"out{t}"], is_output=True)

    def program():
        st["ps"] = 0
        st["pb"] = 0
        st["wi"] = 0
        st["wrel"] = 0
        cnt["ch"] = 0
        held.clear()
        S.op("pool", lambda e: e.memset(small[:], 0.0), writes=["ssb", "rsb"])
        S.dma("sp", g1t[:], g1t_d, writes=["g1t"])
        S.dma("sp", g2t[:], g2t_d, writes=["g2t"])
        S.dma("sp", cw[:], cw_d, writes=["cw"])
        S.dma("sp", zeta[:], zeta_d, writes=["zeta"])
        S.dma("sp", negA[:], alog_d.partition_broadcast(P), writes=["negA"])
        S.dma("sp", dtb[:], dtb_d.partition_broadcast(P), writes=["dtb"])
        S.dma("sp", AR[:, 0:128], gdnc_d[:, 256:384], writes=["AR"])
        S.op("dve", CP(identb[:], AR[:, 0:128]), reads=["AR"], writes=["identb"])
        S.op("act", ACT(negA[:], negA[:], AF.Exp), reads=["negA"], writes=["negA"])
        S.op("dve", TS(negA[:], negA[:], -1.0, ALU.mult), reads=["negA"], writes=["negA"])
        S.op("pool", lambda e: e.memset(Rst[:], 0.0), writes=[f"R{h}" for h in range(RH)])
        S.op("pool", lambda e: e.memset(Sst[:], 0.0), writes=[f"S{h}" for h in range(GH)])
        S.op("pool", lambda e: e.memset(halo[:], 0.0), writes=[f"halo{c}" for c in range(3 * GH)])
        S.fence(["g1t", "g2t", "cw", "zeta", "negA", "dtb", "identb", "AR"])
        stopped = False
        for bi in range(NB):
            if block(False, bi) == "STOP":
                stopped = True
                break
        for bi in range(NB):
            if stopped or block(True, bi) == "STOP":
                break
        S.finish()

    S.dry = True
    program()
    S.dry = False
    program()
    S.emit()
    es.close()
    nc._dbg_layout = {"g_q": g_q, "g_k": g_k, "g_v": g_v, "g_beta": g_beta, "g_g": g_g, "g_eg": g_eg, "g_bw": g_bw,
                      "g_rhsu": g_rhsu, "g_rhsw": g_rhsw, "g_kdec": g_kdec, "g_G2": g_G2, "g_dm": g_dm, "g_dec": g_dec,
                      "g_attn": g_attn, "g_uw": g_uw,
                      "g_gz": g_gz}
    return nc


def _const_tables(D, TC, j):
    RW = D // 2
    RH = RW // 256
    GH = (D // 2) // 128
    f32 = np.float32
    inv = (f32(ROPE_BASE) ** (-(np.arange(128, dtype=f32)) * f32(2.0 / 256))).astype(f32)
    pos_main = (j * TC + np.arange(TC)).astype(f32)
    pos_pre = (max(j - 1, 0) * TC + np.arange(TC)).astype(f32)
    pos = np.concatenate([pos_pre, pos_main])
    ang = (pos[None, :] * inv[:, None]).astype(f32)
    rope = np.stack([np.cos(ang), np.sin(ang)]).astype(f32)
    idx = np.arange(128)
    c = idx[None, :]
    m = idx[:, None]
    retc = np.zeros((RH, 128, 256), f32)
    zeta = np.zeros((128, RH), f32)
    for h in range(RH):
        lg = np.log1p(-2.0 ** (-5.0 - h))
        same = (c // 64) == (m // 64)
        dt = np.where(same, np.exp(lg * np.abs(c - m)), np.where(c > m, np.exp(lg * (c - m)), 0.0))
        retc[h, :, 0:128] = dt / 16.0
        retc[h, :, 128:256] = np.exp(lg * (idx + 1.0))[None, :]
        zeta[:, h] = np.exp(lg * (127.0 - idx)) / 16.0
    jj = idx[:, None]
    cc = idx[None, :]
    Lc = (jj <= cc).astype(f32)
    Uc = (jj > cc).astype(f32)
    ident = np.eye(128, dtype=f32)
    ones = np.ones((128, 128), f32)
    NEGS = np.where(jj > cc, 0.0, -1e4).astype(f32)
    NEGT = np.where(cc >= jj, 0.0, -1e4).astype(f32)
    gdnc = np.concatenate([Lc, Uc, ident, ones, NEGS, NEGT], axis=1).astype(f32)
    return rope, retc, zeta, gdnc


def _run(x, ln1_gain, w_in, ret_norm_gain, gdn_conv_w, gdn_A_log, gdn_dt_bias, gdn_norm_gain,
         w_out, ln2_gain, w_up, w_down, final_gain, debug=False, stop=False):
    x = np.asarray(x, np.float32)
    B, T, D = x.shape
    DFF = w_up.shape[-1]
    TC = T // 2
    NB = TC // TB
    KC = D // 128
    GH = (D // 2) // 128
    nc = build_program(D, DFF, NB, debug=debug, stop=stop)
    _run.nc = nc
    f = lambda a: np.ascontiguousarray(np.asarray(a, np.float32))
    w_in0, w_out0, w_up0, w_down0 = f(w_in[0]), f(w_out[0]), f(w_up[0]), f(w_down[0])
    g1t = f(np.asarray(ln1_gain[0]).reshape(KC, 128).T)
    g2t = f(np.asarray(ln2_gain[0]).reshape(KC, 128).T)
    cwa = np.asarray(gdn_conv_w[0], np.float32)
    cw = f(cwa.reshape(4, 3 * GH, 128).transpose(2, 0, 1).reshape(128, 12 * GH))
    in_maps = []
    zeros = np.zeros((TC, D), np.float32)
    for c in range(2 * B):
        b, j = divmod(c, 2)
        rope, retc, zeta, gdnc = _const_tables(D, TC, j)
        in_maps.append({
            "xp": f(x[b, 0:TC]) if j == 1 else zeros,
            "xm": f(x[b, j * TC:(j + 1) * TC]),
            "w_in": w_in0, "w_out": w_out0, "w_up": w_up0, "w_down": w_down0,
            "rope": rope, "retc": retc, "zeta": zeta, "gdnc": gdnc,
            "g1t": g1t, "g2t": g2t, "cw": cw,
            "rgain": f(ret_norm_gain[0]), "ggain": f(gdn_norm_gain[0]),
            "alog": f(gdn_A_log[0]), "dtb": f(gdn_dt_bias[0]), "fgain": f(final_gain),
        })
    res = run_bass_kernel_spmd(nc, in_maps, core_ids=list(range(2 * B)))
    out = np.empty((B, T, D), np.float32)
    for c in range(2 * B):
        b, j = divmod(c, 2)
        out[b, j * TC:(j + 1) * TC] = np.asarray(res.results[c]["out"], np.float32)
    if debug:
        return out, [np.asarray(r["dbg"]) for r in res.results]
    return out


def kernel(x, ln1_gain, w_in, ret_norm_gain, gdn_conv_w, gdn_A_log, gdn_dt_bias, gdn_norm_gain,
           w_out, ln2_gain, w_up, w_down, final_gain):
    return _run(x, ln1_gain, w_in, ret_norm_gain, gdn_conv_w, gdn_A_log, gdn_dt_bias, gdn_norm_gain,
                w_out, ln2_gain, w_up, w_down, final_gain)
```
